# Optimizing a Trainium2 kernel written in Bass

```python
import math
import jax, jax.numpy as jnp
from jax import lax
import numpy as np

D_MODEL = 1024
BATCH = 8
SEQ = 2048
DEPTH = 2
DEC_BATCH = 4
DEC_SEQ = 4096
PAST_LEN = 128

HEAD_DIM = 64
BLOCK = 128
A_GROUPS = 8
A_WIDTH = A_GROUPS * HEAD_DIM
B_HEADS = 8
B_KV = 2
WINDOW = 128
C_HEADS = 8
C_KV = 2
ROPE_THETA = 10000.0
GRID_W = 64
N_BUCKETS = 32
MAX_DIST = 128
D_FF = 2816
CONV_W = 3
EPS = 1e-6
N_BRANCH = 3
BR_WIDTH = 512
A_IN = 2 * A_WIDTH
B_IN = (B_HEADS + 2 * B_KV) * HEAD_DIM
C_IN = (C_HEADS + 2 * C_KV) * HEAD_DIM
IN_WIDTH = A_IN + B_IN + C_IN

kernel_name = 'hybrid_gated_parallel_encoder'

f32 = jnp.float32


def rmsnorm(x, g):
    xf = x.astype(f32)
    y = xf * lax.rsqrt(jnp.mean(xf * xf, axis=-1, keepdims=True) + EPS)
    return (y * g.astype(f32)).astype(x.dtype)


def layernorm(x, g, b):
    xf = x.astype(f32)
    mu = jnp.mean(xf, axis=-1, keepdims=True)
    xc = xf - mu
    y = xc * lax.rsqrt(jnp.mean(xc * xc, axis=-1, keepdims=True) + EPS)
    return (y * g.astype(f32) + b.astype(f32)).astype(x.dtype)


def t5_bucket(rel):
    half = N_BUCKETS // 2
    max_exact = half // 2
    ret = jnp.where(rel > 0, half, 0)
    n = jnp.abs(rel)
    nf = jnp.maximum(n, 1).astype(f32)
    large = max_exact + (jnp.log(nf / max_exact) / math.log(MAX_DIST / max_exact)
                         * (half - max_exact)).astype(jnp.int32)
    large = jnp.minimum(large, half - 1)
    return ret + jnp.where(n < max_exact, n, large)


def mixer_a(z, ln_g, ln_b, w_s, b_s):
    bsz, s, _ = z.shape
    nb = s // BLOCK
    z = jax.nn.gelu(z)
    u, v = jnp.split(z, 2, axis=-1)
    v = layernorm(v, ln_g, ln_b).reshape(bsz, nb, BLOCK, A_GROUPS, HEAD_DIM)
    sv = jnp.einsum('gpq,bnqgc->bnpgc', w_s, v) + b_s.T[None, None, :, :, None]
    return u * sv.reshape(bsz, s, A_WIDTH)


def mixer_b(q, k, v, sink, bias, win):
    bsz, s = q.shape[:2]
    nb = s // BLOCK
    g = B_HEADS // B_KV

    def band(t):
        tp = jnp.pad(t, ((0, 0), (BLOCK, BLOCK), (0, 0), (0, 0)))
        tp = tp.reshape(bsz, nb + 2, BLOCK, B_KV, HEAD_DIM)
        return jnp.concatenate([tp[:, :-2], tp[:, 1:-1], tp[:, 2:]], axis=2)

    kb, vb = band(k), band(v)
    qb = q.reshape(bsz, nb, BLOCK, B_KV, g, HEAD_DIM)
    logits = jnp.einsum('bnqkgd,bnjkd->bnkgqj', qb, kb).astype(f32) * (HEAD_DIM ** -0.5)
    logits = logits + bias.reshape(B_KV, g, BLOCK, 3 * BLOCK)
    kpos = (jnp.arange(nb)[:, None] - 1) * BLOCK + jnp.arange(3 * BLOCK)[None, :]
    valid = (kpos >= 0) & (kpos < s)
    mask = win[None] & valid[:, None, :]
    logits = jnp.where(mask[None, :, None, None], logits, -jnp.inf)
    sk = sink.astype(f32).reshape(B_KV, g, 1, 1)
    m = jnp.maximum(jnp.max(logits, axis=-1, keepdims=True), sk)
    p = jnp.exp(logits - m)
    probs = p / (jnp.sum(p, axis=-1, keepdims=True) + jnp.exp(sk - m))
    o = jnp.einsum('bnkgqj,bnjkd->bnqkgd', probs.astype(v.dtype), vb)
    return o.reshape(bsz, s, B_HEADS * HEAD_DIM)


def rope_axis(x, pos):
    m = x.shape[-1] // 2
    inv = ROPE_THETA ** (-jnp.arange(m, dtype=f32) / m)
    ang = pos.astype(f32)[:, None] * inv[None, :]
    cos = jnp.cos(ang)[:, None, :]
    sin = jnp.sin(ang)[:, None, :]
    x1 = x[..., :m].astype(f32)
    x2 = x[..., m:].astype(f32)
    return jnp.concatenate([x1 * cos - x2 * sin, x2 * cos + x1 * sin], axis=-1).astype(x.dtype)


def rope_2d(x, row, col):
    h = HEAD_DIM // 2
    return jnp.concatenate([rope_axis(x[..., :h], row), rope_axis(x[..., h:], col)], axis=-1)


def mixer_c(q, k, v, qg, kg, row, col):
    bsz, s = q.shape[:2]
    nb = s // BLOCK
    g = C_HEADS // C_KV
    q = rope_2d(rmsnorm(q, qg), row, col)
    k = rope_2d(rmsnorm(k, kg), row, col)
    qb = q.reshape(bsz, nb, BLOCK, C_KV, g, HEAD_DIM).transpose(1, 0, 2, 3, 4, 5)
    scale = HEAD_DIM ** -0.5

    def attend(qblk):
        logits = jnp.einsum('bqkgd,bskd->bkgqs', qblk, k).astype(f32) * scale
        p = jax.nn.softmax(logits, axis=-1).astype(v.dtype)
        return jnp.einsum('bkgqs,bskd->bqkgd', p, v)

    o = lax.map(attend, qb)
    return o.transpose(1, 0, 2, 3, 4, 5).reshape(bsz, s, C_HEADS * HEAD_DIM)


def conv_ffn(x, w_up, cw, cb, w_down):
    h = x @ w_up
    hp = jnp.pad(h, ((0, 0), (1, 1), (0, 0)))
    h = hp[:, :-2] * cw[0] + hp[:, 1:-1] * cw[1] + hp[:, 2:] * cw[2] + cb
    gt, val = jnp.split(h, 2, axis=-1)
    return (jax.nn.silu(gt) * val) @ w_down


def trunk(x, bias_b, win, norm1_g, w_in, ln_v_g, ln_v_b, w_spatial, b_spatial, sink,
          q_norm_g, k_norm_g, w_gate, b_gate, w_branch, w_out, norm2_g, w_up, conv_w,
          conv_b, w_down, final_g):
    bsz, s, d = x.shape
    rows = s // GRID_W
    row = jnp.repeat(jnp.arange(rows), GRID_W)
    col = jnp.tile(jnp.arange(GRID_W), rows)
    hb = B_HEADS * HEAD_DIM
    kvb = B_KV * HEAD_DIM
    hc = C_HEADS * HEAD_DIM
    kvc = C_KV * HEAD_DIM
    for l in range(DEPTH):
        xn = rmsnorm(x, norm1_g[l])
        z = xn @ w_in[l]
        za = z[..., :A_IN]
        zb = z[..., A_IN:A_IN + B_IN]
        zc = z[..., A_IN + B_IN:]
        o_a = mixer_a(za, ln_v_g[l], ln_v_b[l], w_spatial[l], b_spatial[l])
        o_b = mixer_b(zb[..., :hb].reshape(bsz, s, B_HEADS, HEAD_DIM),
                      zb[..., hb:hb + kvb].reshape(bsz, s, B_KV, HEAD_DIM),
                      zb[..., hb + kvb:].reshape(bsz, s, B_KV, HEAD_DIM),
                      sink[l], bias_b, win)
        o_c = mixer_c(zc[..., :hc].reshape(bsz, s, C_HEADS, HEAD_DIM),
                      zc[..., hc:hc + kvc].reshape(bsz, s, C_KV, HEAD_DIM),
                      zc[..., hc + kvc:].reshape(bsz, s, C_KV, HEAD_DIM),
                      q_norm_g[l], k_norm_g[l], row, col)
        o = jnp.stack([o_a, o_b, o_c], axis=2)
        y = jnp.einsum('bsne,ned->bsnd', o, w_branch[l])
        gates = jax.nn.sigmoid(xn @ w_gate[l] + b_gate[l]).reshape(bsz, s, N_BRANCH, d)
        merged = jnp.sum(gates * y, axis=2)
        x = x + merged @ w_out[l]
        x = x + conv_ffn(rmsnorm(x, norm2_g[l]), w_up[l], conv_w[l], conv_b[l], w_down[l])
    return rmsnorm(x, final_g)


def setup_inputs(seed: int = 0) -> dict:
    key = jax.random.key(seed)
    ks = jax.random.split(key, 24)

    def nrm(k, shape, scale):
        return jax.random.normal(k, shape, f32) * scale

    L, D = DEPTH, D_MODEL
    return {
        'x_prompt': nrm(ks[0], (BATCH, SEQ, D), 1.0),
        'x_sample': nrm(ks[1], (DEC_BATCH, DEC_SEQ, D), 1.0),
        'rel_bias': nrm(ks[2], (N_BUCKETS, B_HEADS), 0.5),
        'norm1_g': 1.0 + nrm(ks[3], (L, D), 0.1),
        'w_in': nrm(ks[4], (L, D, IN_WIDTH), D ** -0.5),
        'ln_v_g': 1.0 + nrm(ks[5], (L, A_WIDTH), 0.1),
        'ln_v_b': nrm(ks[6], (L, A_WIDTH), 0.05),
        'w_spatial': nrm(ks[7], (L, A_GROUPS, BLOCK, BLOCK), BLOCK ** -0.5),
        'b_spatial': 1.0 + nrm(ks[8], (L, A_GROUPS, BLOCK), 0.1),
        'sink': nrm(ks[9], (L, B_HEADS), 0.5),
        'q_norm_g': 1.0 + nrm(ks[10], (L, HEAD_DIM), 0.1),
        'k_norm_g': 1.0 + nrm(ks[11], (L, HEAD_DIM), 0.1),
        'w_gate': nrm(ks[12], (L, D, N_BRANCH * D), D ** -0.5),
        'b_gate': nrm(ks[13], (L, N_BRANCH * D), 0.05),
        'w_branch': nrm(ks[14], (L, N_BRANCH, BR_WIDTH, D), BR_WIDTH ** -0.5),
        'w_out': nrm(ks[15], (L, D, D), D ** -0.5),
        'norm2_g': 1.0 + nrm(ks[16], (L, D), 0.1),
        'w_up': nrm(ks[17], (L, D, 2 * D_FF), D ** -0.5),
        'conv_w': nrm(ks[18], (L, CONV_W, 2 * D_FF), CONV_W ** -0.5),
        'conv_b': nrm(ks[19], (L, 2 * D_FF), 0.05),
        'w_down': nrm(ks[20], (L, D_FF, D), D_FF ** -0.5),
        'final_g': 1.0 + nrm(ks[21], (D,), 0.1),
    }


def reference(x_prompt, x_sample, rel_bias, norm1_g, w_in, ln_v_g, ln_v_b, w_spatial,
              b_spatial, sink, q_norm_g, k_norm_g, w_gate, b_gate, w_branch, w_out,
              norm2_g, w_up, conv_w, conv_b, w_down, final_g):
    qpos = jnp.arange(BLOCK)[:, None]
    jpos = jnp.arange(3 * BLOCK)[None, :]
    rel = jpos - BLOCK - qpos
    win = jnp.abs(rel) <= WINDOW
    bias_b = rel_bias[t5_bucket(rel)].transpose(2, 0, 1).astype(f32)
    y_prompt = trunk(x_prompt, bias_b, win, norm1_g, w_in, ln_v_g, ln_v_b, w_spatial,
                     b_spatial, sink, q_norm_g, k_norm_g, w_gate, b_gate, w_branch, w_out,
                     norm2_g, w_up, conv_w, conv_b, w_down, final_g)
    y_sample = trunk(x_sample, bias_b, win, norm1_g, w_in, ln_v_g, ln_v_b, w_spatial,
                     b_spatial, sink, q_norm_g, k_norm_g, w_gate, b_gate, w_branch, w_out,
                     norm2_g, w_up, conv_w, conv_b, w_down, final_g)
    return (y_prompt, y_sample)
```

```python
import math
from contextlib import ExitStack

import numpy as np
import concourse.bass as bass
import concourse.mybir as mybir
from concourse.bass_utils import run_bass_kernel_spmd

F32 = mybir.dt.float32
BF16 = mybir.dt.bfloat16
AF = mybir.ActivationFunctionType
ALU = mybir.AluOpType

D = 1024
KD = 8
HD = 64
DIN = 2560
DG = 3072
DFF = 2816
NFC = 22
EPS = 1e-6
NEG = -30000.0
T = 512
FT = 410
NS = 3


class _Op:
    __slots__ = ("eng", "fn", "idx", "deps", "dma_waits", "is_dma", "semkey", "cum", "signal", "count")


class Prog:
    ENGS = ("pe", "act", "dve", "pool", "sp")
    GRAN = 512
    CH = 6000

    def __init__(self):
        self.ops = {e: [] for e in self.ENGS}
        self.res = {}
        self.dma_cum = {}

    def _keys(self, a):
        if not isinstance(a, bass.AP):
            return [a]
        sp = str(a.space)
        name = a.tensor.name
        es = mybir.dt.size(a.dtype)
        dims = a.ap
        pstep = dims[0][0]
        if sp == "PSUM":
            gran = 2048
        else:
            gran = self.GRAN
        foff = (a.offset % pstep) if pstep > 0 else a.offset
        starts = [foff]
        for (st, cnt) in dims[1:-1]:
            starts = [s + i * st for s in starts for i in range(cnt)]
        lst, lcnt = dims[-1] if len(dims) > 1 else (1, 1)
        ln = (abs(lst) * (lcnt - 1) + 1)
        ks = set()
        for s in starts:
            b0 = (s * es) // gran
            b1 = ((s + ln) * es - 1) // gran
            for b in range(b0, b1 + 1):
                ks.add((name, b))
        return list(ks)

    def op(self, eng, fn, reads=(), writes=(), dma=None):
        if getattr(self, "frozen", False):
            return None
        o = _Op()
        o.eng = eng
        o.fn = fn
        o.idx = len(self.ops[eng])
        o.deps = set()
        o.dma_waits = {}
        o.is_dma = dma is not None
        o.semkey = dma
        o.signal = False
        o.count = 0
        rk = [k for r in reads for k in self._keys(r)]
        wk = [k for w in writes for k in self._keys(w)]
        for k in rk:
            st = self.res.get(k)
            if st is not None and st[0] is not None:
                o.deps.add(st[0])
        for k in wk:
            st = self.res.get(k)
            if st is not None:
                if st[0] is not None:
                    o.deps.add(st[0])
                for r in st[1]:
                    o.deps.add(r)
        o.deps.discard(o)
        for d in o.deps:
            if d.is_dma:
                o.dma_waits[d.semkey] = self.dma_cum[d.semkey]
        if o.is_dma:
            self.dma_cum[dma] = self.dma_cum.get(dma, 0) + 16
            o.cum = self.dma_cum[dma]
        for k in rk:
            st = self.res.setdefault(k, [None, []])
            if not o.is_dma:
                st[1] = [r for r in st[1] if r.is_dma or r.eng != eng]
            st[1].append(o)
        for k in wk:
            self.res[k] = [o, []]
        self.ops[eng].append(o)
        return o

    def finalize(self):
        for e in self.ENGS:
            for o in self.ops[e]:
                keep = []
                for d in o.deps:
                    if d.is_dma:
                        continue
                    if d.eng == o.eng and not o.is_dma:
                        if o.eng == "pe":
                            continue
                        if o.idx - d.idx > 3:
                            continue
                    d.signal = True
                    keep.append(d)
                o.deps = keep
        for e in self.ENGS:
            c = 0
            for o in self.ops[e]:
                if o.signal:
                    c += 1
                    o.count = c
        return {e: max([o.count for o in self.ops[e]] + [0]) for e in self.ENGS}


def _pkn(w, kc):
    n = w.shape[1]
    return np.ascontiguousarray(w.reshape(kc, 128, n).transpose(1, 0, 2))


def _t5_bucket_np(rel):
    half = 16
    max_exact = 8
    ret = np.where(rel > 0, half, 0)
    n = np.abs(rel)
    nf = np.maximum(n, 1).astype(np.float32)
    large = max_exact + (np.log(nf / np.float32(max_exact)) / np.float32(math.log(128 / max_exact))
                         * np.float32(half - max_exact)).astype(np.int32)
    large = np.minimum(large, half - 1)
    return ret + np.where(n < max_exact, n, large)


_QPERM = np.concatenate([np.concatenate([j * 64 + np.arange(64), (4 + j) * 64 + np.arange(64)]) for j in range(4)])


def _pcol_layout(L):
    off = {}
    c = 0

    def add(name, n):
        nonlocal c
        off[name] = c
        c += n

    add("zero", 1)
    add("eps", 1)
    add("cmask", 1)
    add("m01", 1)
    for l in range(L):
        add(f"n1g{l}", 8)
        add(f"n2g{l}", 8)
        add(f"bg{l}", 24)
        add(f"cw0{l}", 44)
        add(f"cw1{l}", 44)
        add(f"cw2{l}", 44)
        add(f"cb{l}", 44)
        add(f"qg{l}", 1)
        add(f"kg{l}", 1)
    add("fing", 8)
    return off, c


def _shared_inputs(p, L):
    f = np.float32
    out = {}
    wl = {k: [] for k in ("w_in", "w_gate", "w_br", "w_out", "w_up", "w_down")}
    for l in range(L):
        wi = p["w_in"][l]
        cat = np.concatenate([wi[:, 0:512], wi[:, 1024:1536][:, _QPERM], wi[:, 1792:2304][:, _QPERM],
                              wi[:, 512:1024], wi[:, 1536:1664], wi[:, 2304:2432], wi[:, 1664:1792], wi[:, 2432:2560]], axis=1)
        wl["w_in"].append(_pkn(cat, 8).reshape(128, 8, 5, 512).transpose(0, 2, 1, 3).reshape(128, 5, 4096))
        wl["w_gate"].append(_pkn(p["w_gate"][l], 8).reshape(128, 8, 6, 512).transpose(0, 2, 1, 3).reshape(128, 6, 4096))
        rows = np.concatenate([p["w_branch"][l][0], p["w_branch"][l][1][_QPERM], p["w_branch"][l][2][_QPERM]], axis=0)
        wl["w_br"].append(_pkn(rows, 12).reshape(128, 3, 4, 2, 512).transpose(0, 1, 3, 2, 4).reshape(128, 6, 2048))
        wl["w_out"].append(_pkn(p["w_out"][l], 8).reshape(128, 8, 2, 512).transpose(0, 2, 1, 3).reshape(128, 2, 4096))
        up = _pkn(p["w_up"][l], 8)
        gv = np.stack([up[:, :, 0:DFF].reshape(128, 8, NFC, 128), up[:, :, DFF:].reshape(128, 8, NFC, 128)], axis=3)
        wl["w_up"].append(gv.transpose(0, 2, 1, 3, 4).reshape(128, NFC, 2048))
        wl["w_down"].append(_pkn(p["w_down"][l], NFC).reshape(128, NFC, 8, 128).transpose(0, 2, 1, 3).reshape(128, 8, NFC * 128))
    for k in wl:
        out[k] = np.ascontiguousarray(np.stack(wl[k])).astype(f)
    out["lngb"] = np.stack([np.stack([np.broadcast_to(p["ln_v_g"][l], (128, 512)),
                                      np.broadcast_to(p["ln_v_b"][l], (128, 512))]) for l in range(L)]).astype(f)
    bs = np.zeros((L, 128, 4, 128), f)
    for l in range(L):
        for j in range(4):
            bs[l, 0:64, j, :] = p["b_spatial"][l][2 * j][None, :]
            bs[l, 64:128, j, :] = p["b_spatial"][l][2 * j + 1][None, :]
    out["bS"] = bs
    out["wsT"] = np.ascontiguousarray(np.stack([p["w_spatial"][l].transpose(2, 0, 1) for l in range(L)])).astype(f)
    sk = np.zeros((L, 128, 2, 512), f)
    for l in range(L):
        for g in range(2):
            for hl in range(4):
                sk[l, :, g, hl * 128:(hl + 1) * 128] = p["sink"][l][g * 4 + hl]
    out["sinkb"] = sk
    out["ident"] = np.eye(128, dtype=f)
    out["onesm"] = np.full((128, 128), 1.0 / 1024.0, f)
    blk = np.zeros((128, 128), f)
    blk[0:64, 0:64] = 1.0 / 64.0
    blk[64:128, 64:128] = 1.0 / 64.0
    out["blk"] = blk
    R = np.zeros((128, 128), f)
    for base in (0, 32, 64, 96):
        for e in range(16):
            R[base + e + 16, base + e] = -1.0
            R[base + e, base + e + 16] = 1.0
    out["rotm"] = R
    oh = np.zeros((33, 3, 256), f)
    for c in range(3):
        s = np.arange(255)
        rel = 128 * (c - 1) + 127 - s
        b = _t5_bucket_np(rel.astype(np.int32))
        oh[b, c, s] = 1.0
        oh[32, c, s] = np.where(np.abs(rel) <= 128, 0.0, NEG)
    out["ohm"] = oh
    rr = np.ones((33, 8, 128), f)
    rr[0:32] = p["rel_bias"][:, :, None]
    out["relrep"] = rr
    return out


def _pcol_host(p, L, two_seq):
    off, n = _pcol_layout(L)
    pc = np.zeros((128, n), np.float32)
    pc[:, off["eps"]] = EPS
    pc[:, off["cmask"]] = NEG if two_seq else 0.0
    pc[:, off["m01"]] = 0.0 if two_seq else 1.0

    def col8(v):
        return v.reshape(-1, 128).T

    for l in range(L):
        pc[:, off[f"n1g{l}"]:off[f"n1g{l}"] + 8] = col8(p["norm1_g"][l])
        pc[:, off[f"n2g{l}"]:off[f"n2g{l}"] + 8] = col8(p["norm2_g"][l])
        pc[:, off[f"bg{l}"]:off[f"bg{l}"] + 24] = col8(p["b_gate"][l])
        for j in range(3):
            pc[:, off[f"cw{j}{l}"]:off[f"cw{j}{l}"] + 44] = col8(p["conv_w"][l][j])
        pc[:, off[f"cb{l}"]:off[f"cb{l}"] + 44] = col8(p["conv_b"][l])
        pc[:, off[f"qg{l}"]] = np.tile(p["q_norm_g"][l], 2)
        pc[:, off[f"kg{l}"]] = np.tile(p["k_norm_g"][l], 2)
    pc[:, off["fing"]:off["fing"] + 8] = col8(p["final_g"])
    return pc


def _rope_host(NT, two_seq):
    pos = np.arange(NT)
    if two_seq:
        pos = pos % (NT // 2)
    row = (pos // 64).astype(np.float32)
    col = (pos % 64).astype(np.float32)
    inv = (np.float32(10000.0) ** (-np.arange(16, dtype=np.float32) / np.float32(16))).astype(np.float32)
    tab = np.zeros((2, 128, NT), np.float32)
    for pp in range(128):
        d = pp % 64
        ax = row if d < 32 else col
        i = (d % 32) % 16
        ang = (ax * inv[i]).astype(np.float32)
        tab[0, pp] = np.cos(ang)
        tab[1, pp] = np.sin(ang)
    return tab


def build_program(NT, L):
    NB = NT // 128
    NTILE = NT // T
    H = NT // 2
    HB = NB // 2
    off, NPC = _pcol_layout(L)

    nc = bass.Bass("TRN2", target_bir_lowering=False)
    P = Prog()
    import os
    STOP = int(os.environ.get("KSTOP", "99"))

    def ckpt(n):
        if STOP == n:
            P.frozen = True

    def din(name, shape, dt=F32):
        return nc.dram_tensor(name, list(shape), dt, kind="ExternalInput")

    x_d = din("x", [NT, D])
    win_d = din("w_in", [L, 128, 5, 4096])
    wg_d = din("w_gate", [L, 128, 6, 4096])
    wbr_d = din("w_br", [L, 128, 6, 2048])
    wo_d = din("w_out", [L, 128, 2, 4096])
    wup_d = din("w_up", [L, 128, NFC, 2048])
    wdn_d = din("w_down", [L, 128, 8, NFC * 128])
    lngb_d = din("lngb", [L, 2, 128, 512])
    bS_d = din("bS", [L, 128, 4, 128])
    wsT_d = din("wsT", [L, 128, 8, 128])
    sinkb_d = din("sinkb", [L, 128, 2, 512])
    ident_d = din("ident", [128, 128])
    onesm_d = din("onesm", [128, 128])
    blk_d = din("blk", [128, 128])
    rotm_d = din("rotm", [128, 128])
    ohm_d = din("ohm", [33, 3, 256])
    relrep_d = din("relrep", [33, 8, 128])
    pcol_d = din("pcol", [128, NPC])
    rope_d = din("rope", [2, 128, NT])
    y_d = nc.dram_tensor("y", [NT, D], F32, kind="ExternalOutput")

    def dint(name, shape, dt):
        return nc.dram_tensor(name, list(shape), dt, kind="Internal")

    winb = dint("winb", [L, 128, 5, 4096], BF16)
    wgb = dint("wgb", [L, 128, 6, 4096], BF16)
    wbrb = dint("wbrb", [L, 128, 6, 2048], BF16)
    wob = dint("wob", [L, 128, 2, 4096], BF16)
    wupb = dint("wupb", [L, 128, NFC, 2048], BF16)
    wdnb = dint("wdnb", [L, 128, 8, NFC * 128], BF16)
    xsA = dint("xsA", [128, 8, NT], F32)
    xsB = dint("xsB", [128, 8, NT], F32)
    tabx = dint("tabx", [3, 128, 8, 256], F32)

    es = ExitStack()

    def sb(name, shape, dt):
        return es.enter_context(nc.sbuf_tensor("sb_" + name, list(shape), dt))

    KbT = sb("KbT", [128, NT], BF16)
    KcT = sb("KcT", [128, NT], BF16)
    Vb = sb("Vb", [128, NB, 256], BF16)
    Vc = sb("Vc", [128, NB, 256], BF16)
    biasHL = sb("biasHL", [128, 12, 512], BF16)
    identb = sb("identb", [128, 128], BF16)
    pcol = sb("pcol", [128, NPC], F32)
    ident = sb("ident", [128, 128], F32)
    onesm = sb("onesm", [128, 128], BF16)
    blkb = sb("blkb", [128, 128], BF16)
    rotb = sb("rotb", [128, 128], BF16)
    lnG = sb("lnG", [128, 512], F32)
    lnB = sb("lnB", [128, 512], F32)
    bS = sb("bS", [128, 4, 128], F32)
    wsT = sb("wsT", [128, 8, 128], BF16)
    sinke = sb("sinke", [128, 2, 512], F32)
    ropeC = sb("ropeC", [128, 512], F32)
    ropeS = sb("ropeS", [128, 512], F32)
    xTd = sb("xTd", [128, 2, 4096], F32)
    cur = {"p": 0}

    def xTv(p=None):
        p = cur["p"] if p is None else p
        return xTd[:, p, :].rearrange("q (a b) -> q a b", b=512)
    xn = sb("xn", [128, 8, 512], BF16)
    sq = sb("sq", [128, 2, 512], BF16)
    rstd = sb("rstd", [128, 512], F32)
    wbuf = sb("wbuf", [128, NS * 4096], BF16)
    stat = sb("stat", [128, 32], F32)
    ARENA = 58 * 1024 // 2
    arena = sb("arena", [128, ARENA], BF16)
    ps = es.enter_context(nc.psum_tensor("ps", [128, 8, 512], F32))

    def av(byte_off, nbytes, dt, shape3=None):
        a = arena[:, byte_off // 2:(byte_off + nbytes) // 2]
        if dt == F32:
            a = a.bitcast(F32)
        if shape3 is not None:
            a = a.rearrange("p (a b) -> p a b", b=shape3)
        return a

    K = 1024
    uT = av(0, 8 * K, F32, 512)
    vg = av(8 * K, 4 * K, F32, 512)
    vln = av(12 * K, 4 * K, BF16, 512)
    rt = av(16 * K, 8 * K, F32, 512)
    knb = av(24 * K, 1 * K, BF16)
    sqc = av(25 * K, 1 * K, BF16)
    mg = av(0, 8 * K, BF16, 512)
    gt = av(8 * K, 4 * K, F32, 512)
    acc = av(12 * K, 8 * K, F32, 512)
    qbT = av(26 * K, 4 * K, BF16, 512)
    qcT = av(30 * K, 4 * K, BF16, 512)
    oT = av(34 * K, 12 * K, BF16, 512)
    tmpS = av(46 * K, 4 * K, F32, 512)
    PT = av(50 * K, 4 * K, BF16, 512)
    rc = av(54 * K, 4 * K, F32, 512)
    actb = av(0, 22 * K, BF16, 512)
    hc = av(22 * K, 8 * K, F32, 512)
    sg = av(30 * K, 4 * K, F32, 512)
    yT = av(0, 16 * K, F32, 512)
    yo = av(16 * K, 8 * K, F32, 1024)
    xin = av(26 * K, 16 * K, F32, 1024)
    trep = av(0, 8 * K, F32, 256)
    biasT = av(26 * K, 12 * K, F32, 512)
    bh32 = av(38 * K, 4 * K, F32, 512)
    relrep_s = av(8 * K, 4 * K, F32, 128)
    ohm_s = av(12 * K, 3 * K, F32, 256)

    def pc(name, i=0):
        c = off[name] + i
        return pcol[:, c:c + 1]

    rot_state = {"L8": 0, "S": 0, "O": 0, "L": 0}

    def bank(role):
        sets = {"L8": [0, 1, 2, 3, 4, 5, 6, 7], "S": [0, 1, 2, 3], "O": [4, 5], "L": [6, 7]}[role]
        b = sets[rot_state[role] % len(sets)]
        rot_state[role] += 1
        return b

    def dma(q, out, in_, semkey, reads=(), writes=()):
        return P.op(q, lambda e, out=out, in_=in_: e.dma_start(out=out, in_=in_), reads=reads, writes=writes, dma=semkey)

    def mm(out, lhsT, rhs, start, stop):
        return P.op("pe", lambda e: e.matmul(out, lhsT, rhs, start=start, stop=stop), reads=[lhsT, rhs], writes=[out])

    def tr(out, in_):
        return P.op("pe", lambda e: e.transpose(out, in_, ident[:]), reads=[in_, ident[:]], writes=[out])

    def act(out, in_, func, bias=None, scale=1.0, extra_reads=()):
        rd = [in_] + list(extra_reads)
        if isinstance(bias, bass.AP):
            rd.append(bias)
        if isinstance(scale, bass.AP):
            rd.append(scale)
        kw = {}
        if bias is not None:
            kw["bias"] = bias
        return P.op("act", lambda e: e.activation(out=out, in_=in_, func=func, scale=scale, **kw), reads=rd, writes=[out])

    def tt(eng, out, in0, in1, op):
        return P.op(eng, lambda e: e.tensor_tensor(out=out, in0=in0, in1=in1, op=op), reads=[in0, in1], writes=[out])

    def ts(eng, out, in0, s1, s2, op0, op1=None):
        rd = [in0] + [s for s in (s1, s2) if isinstance(s, bass.AP)]
        if op1 is None:
            return P.op(eng, lambda e: e.tensor_scalar(out=out, in0=in0, scalar1=s1, scalar2=None, op0=op0), reads=rd, writes=[out])
        return P.op(eng, lambda e: e.tensor_scalar(out=out, in0=in0, scalar1=s1, scalar2=s2, op0=op0, op1=op1), reads=rd, writes=[out])

    def stt(out, in0, scalar, in1, op0, op1):
        rd = [in0, in1] + ([scalar] if isinstance(scalar, bass.AP) else [])
        return P.op("dve", lambda e: e.scalar_tensor_tensor(out=out, in0=in0, scalar=scalar, in1=in1, op0=op0, op1=op1), reads=rd, writes=[out])

    def cp(eng, out, in_):
        if eng == "act":
            return act(out, in_, AF.Copy)
        return P.op(eng, lambda e: e.tensor_copy(out=out, in_=in_), reads=[in_], writes=[out])

    def recip(out, in_):
        return P.op("dve", lambda e: e.reciprocal(out=out, in_=in_), reads=[in_], writes=[out])

    wstate = {"h": 0}
    NH = NS * 2

    def walloc(nel):
        nh = (nel + 2047) // 2048
        if wstate["h"] + nh > NH:
            wstate["h"] = 0
        h = wstate["h"]
        wstate["h"] = (h + nh) % NH
        return h

    def wload(nel, dstb, l, nm, blk):
        h = walloc(nel)
        dst_ap = wbuf[:, h * 2048:h * 2048 + nel]
        dma("sp", dst_ap, dstb[l, :, blk, :], ("w", h), reads=[("wbf", nm, l, blk)], writes=[dst_ap])
        return dst_ap

    conv_chunks = []
    for l in range(L):
        conv_chunks += [(win_d, winb, l, "win", 4, 5), (win_d, winb, l, "win", 0, 2), (win_d, winb, l, "win", 2, 4),
                        (wg_d, wgb, l, "wg", 0, 3), (wg_d, wgb, l, "wg", 3, 6), (wbr_d, wbrb, l, "wbr", 0, 6),
                        (wo_d, wob, l, "wo", 0, 2),
                        (wup_d, wupb, l, "wup", 0, 6), (wup_d, wupb, l, "wup", 6, 12), (wup_d, wupb, l, "wup", 12, 18),
                        (wup_d, wupb, l, "wup", 18, 22), (wdn_d, wdnb, l, "wdn", 0, 4), (wdn_d, wdnb, l, "wdn", 4, 8)]
    cstate = {"n": 0}

    def pump(n):
        for _ in range(n):
            i = cstate["n"]
            if i >= len(conv_chunks):
                return
            src, dst, l, nm, k0, k1 = conv_chunks[i]
            rd = [("cvdone", i - 2)] if i >= 2 else []
            dma("pool", dst[l, :, k0:k1, :], src[l, :, k0:k1, :], ("cv", i), reads=rd,
                writes=[("wbf", nm, l, k) for k in range(k0, k1)] + [("cvdone", i)])
            cstate["n"] += 1

    def ensure(nm, l):
        last = max(i for i, c in enumerate(conv_chunks) if c[3] == nm and c[2] == l)
        while cstate["n"] <= last:
            pump(1)

    CPL = 13

    dma("sp", pcol[:], pcol_d[:, :], "c_pcol", writes=[pcol[:]])
    dma("sp", ident[:], ident_d[:, :], "c_ident", writes=[ident[:]])
    dma("pool", onesm[:], onesm_d[:, :], "c_ones", writes=[onesm[:]])
    dma("pool", blkb[:], blk_d[:, :], "c_blk", writes=[blkb[:]])
    dma("pool", rotb[:], rotm_d[:, :], "c_rot", writes=[rotb[:]])
    ckpt(0)
    ensure("win", 0)
    ckpt(1)
    dma("sp", relrep_s[0:33, :, :], relrep_d[:, :, :], "c_rel", writes=[relrep_s[0:33, :, :]])
    dma("sp", ohm_s[0:33, :, :], ohm_d[:, :, :], "c_ohm", writes=[ohm_s[0:33, :, :]])
    for c in range(3):
        for h in range(8):
            b = bank("L8")
            mm(ps[:, b, 0:256], relrep_s[0:33, h, :], ohm_s[0:33, c, :], True, True)
            cp("dve" if h % 2 == 0 else "act", trep[:, h, :], ps[:, b, 0:256])
        dma("sp", tabx[c, :, :, :], trep[:, :, :], "tabx", reads=[trep[:, :, :]], writes=[("tabx", c)])
        for g in range(2):
            src = bass.AP(tabx, c * 128 * 2048 + g * 4 * 256 + 127, [[2047, 128], [256, 4], [1, 128]])
            dst = biasT[:, c * 2 + g, :].rearrange("p (a b) -> p a b", b=128)
            dma("sp", dst, src, "biasT", reads=[("tabx", c)], writes=[biasT[:, c * 2 + g, :]])
    cp("dve", identb[:], ident[:])
    for i in range(6):
        cp("act", biasHL[:, i, :], biasT[:, i, :])
        cp("dve", bh32[:, 0, :], biasHL[:, i, :])
        tt("dve", bh32[:, 1, :], biasT[:, i, :], bh32[:, 0, :], ALU.subtract)
        cp("act", biasHL[:, 6 + i, :], bh32[:, 1, :])
    for vv in (Vb, Vc):
        for g in range(2):
            a = vv[:, :, g * 128 + 64:g * 128 + 128]
            P.op("pool", lambda e, a=a: e.memset(a, 1.0), writes=[a])
    ckpt(2)

    def rmsnorm(ncols, gname, out3, out_f32=False):
        xT = xTv()
        b = bank("L8")
        for k in range(8):
            s = sq[:, k % 2, 0:ncols]
            xk = xT[:, k, 0:ncols]
            if k % 2 == 0:
                tt("pool", s, xk, xk, ALU.mult)
            else:
                act(s, xk, AF.Square)
            mm(ps[:, b, 0:ncols], onesm[:], s, k == 0, k == 7)
        act(rstd[:, 0:ncols], ps[:, b, 0:ncols], AF.Ln, bias=pc("eps"))
        act(rstd[:, 0:ncols], rstd[:, 0:ncols], AF.Exp, scale=-0.5)
        for k in range(8):
            stt(out3[:, k, 0:ncols], xT[:, k, 0:ncols], pc(gname, k), rstd[:, 0:ncols], ALU.mult, ALU.mult)

    sq8 = av(34 * K, 8 * K, BF16, 512)
    tmpS4 = av(16 * K, 8 * K, F32, 512)

    sq8b = av(26 * K, 8 * K, BF16, 512)

    def norm_sq(p, ncols, sqb=None):
        sqb = sq8 if sqb is None else sqb
        xT = xTv(p)
        for k in range(8):
            s_ = sqb[:, k, 0:ncols]
            xk = xT[:, k, 0:ncols]
            if k % 2 == 0:
                tt("pool", s_, xk, xk, ALU.mult)
            else:
                act(s_, xk, AF.Square)

    def norm_fin(p, ncols, gname, out3, sqb=None):
        sqb = sq8 if sqb is None else sqb
        xT = xTv(p)
        b = bank("L8")
        for k in range(8):
            mm(ps[:, b, 0:ncols], onesm[:], sqb[:, k, 0:ncols], k == 0, k == 7)
        act(rstd[:, 0:ncols], ps[:, b, 0:ncols], AF.Ln, bias=pc("eps"))
        act(rstd[:, 0:ncols], rstd[:, 0:ncols], AF.Exp, scale=-0.5)
        for k in range(8):
            stt(out3[:, k, 0:ncols], xT[:, k, 0:ncols], pc(gname, k), rstd[:, 0:ncols], ALU.mult, ALU.mult)

    def qknorm_rope(src, gcol, out, fill=None):
        def f(n):
            if fill is not None:
                for _ in range(n):
                    fill()
        cp("dve", rt[:, 0, :], src)
        act(sqc, rt[:, 0, :], AF.Square)
        f(1)
        b = bank("L8")
        mm(ps[:, b, :], blkb[:], sqc, True, True)
        act(rt[:, 1, :], ps[:, b, :], AF.Ln, bias=pc("eps"))
        act(rt[:, 1, :], rt[:, 1, :], AF.Exp, scale=-0.5)
        stt(knb, rt[:, 0, :], gcol, rt[:, 1, :], ALU.mult, ALU.mult)
        stt(rt[:, 2, :], rt[:, 0, :], gcol, rt[:, 1, :], ALU.mult, ALU.mult)
        f(1)
        b2 = bank("L8")
        mm(ps[:, b2, :], rotb[:], knb, True, True)
        tt("pool", rt[:, 3, :], rt[:, 2, :], ropeC[:], ALU.mult)
        tt("dve", rt[:, 0, :], ps[:, b2, :], ropeS[:], ALU.mult)
        tt("pool", out, rt[:, 3, :], rt[:, 0, :], ALU.add)

    def load_rope(t):
        dma("sp", ropeC[:], rope_d[0, :, t * T:(t + 1) * T], "ropeC", writes=[ropeC[:]])
        dma("sp", ropeS[:], rope_d[1, :, t * T:(t + 1) * T], "ropeS", writes=[ropeS[:]])

    def load_xT(src, name, lo, ncols, p=None):
        p = cur["p"] if p is None else p
        xT = xTv(p)
        dma("sp", xT[:, :, 0:ncols], src[:, :, lo:lo + ncols], ("xT", p), reads=xs_keys(name, lo, lo + ncols), writes=[xT[:, :, 0:ncols]])

    deferred = []

    def flush():
        while deferred:
            deferred.pop(0)()

    def xs_keys(name, lo, hi):
        return [(name, i) for i in range(lo // 2, (hi - 1) // 2 + 1)]

    for l in range(L):
        dma("sp", lnG[:], lngb_d[l, 0, :, :], "lnG", writes=[lnG[:]])
        dma("sp", lnB[:], lngb_d[l, 1, :, :], "lnB", writes=[lnB[:]])
        dma("sp", bS[:], bS_d[l, :, :, :], "bS", writes=[bS[:]])
        dma("pool", wsT[:], wsT_d[l, :, :, :], "wsT", writes=[wsT[:]])
        dma("sp", sinke[:], sinkb_d[l, :, :, :], "sinke", writes=[sinke[:]])
        act(sinke[:], sinke[:], AF.Exp)
        ckpt(20)

        ensure("win", l)
        wkv = wload(4096, winb, l, "win", 4).rearrange("p (a b) -> p a b", b=512)
        ckpt(21)
        if l > 0:
            load_xT(xsA, "xsA", 0, T)
        for t in range(NTILE):
            c0 = t * T
            xT = xTv()
            if l == 0:
                pump(int(os.environ.get("KPUMP", (CPL + NTILE - 1) // NTILE)))
            ckpt(30)
            if l == 0:
                for bq in range(4):
                    xi = xin[:, bq, :]
                    dma("sp", xi, x_d[c0 + bq * 128:c0 + (bq + 1) * 128, :], ("xin", bq), writes=[xi])
                    for hf in range(2):
                        b = bank("L8")
                        for kk in range(4):
                            k = hf * 4 + kk
                            tr(ps[:, b, kk * 128:(kk + 1) * 128], xi[:, k * 128:(k + 1) * 128])
                        cp("dve" if hf == 0 else "act", xT[:, hf * 4:hf * 4 + 4, bq * 128:(bq + 1) * 128],
                           ps[:, b, :].rearrange("p (a b) -> p a b", b=128))
                dma("sp", xsA[:, :, c0:c0 + T], xT[:, :, :], ("xTst", cur["p"]), reads=[xT[:, :, :]], writes=xs_keys("xsA", c0, c0 + T))
            ckpt(31)
            load_rope(t)
            rmsnorm(T, f"n1g{l}", xn)
            if l > 0 and t + 1 < NTILE:
                load_xT(xsA, "xsA", c0 + T, T, 1 - cur["p"])
            ckpt(32)
            fa = []

            def unit_kb(c0=c0):
                b = bank("L8")
                for k in range(8):
                    mm(ps[:, b, :], wkv[:, k, 0:128], xn[:, k, :], k == 0, k == 7)
                cp("act", KbT[:, c0:c0 + T], ps[:, b, :])

            def unit_v(bq, t=t):
                nb = t * 4 + bq
                b = bank("L8")
                for k in range(8):
                    mm(ps[:, b, 0:256], xn[:, k, bq * 128:(bq + 1) * 128], wkv[:, k, 256:512], k == 0, k == 7)
                cp("dve", Vb[:, nb, :].rearrange("p (g c) -> p g c", c=128)[:, :, 0:64],
                   ps[:, b, 0:128].rearrange("p (g c) -> p g c", c=64))
                cp("dve", Vc[:, nb, :].rearrange("p (g c) -> p g c", c=128)[:, :, 0:64],
                   ps[:, b, 128:256].rearrange("p (g c) -> p g c", c=64))

            fa.append(unit_kb)
            for bq in range(4):
                fa.append(lambda bq=bq: unit_v(bq))

            def fillA():
                if fa:
                    fa.pop(0)()

            b = bank("L8")
            for k in range(8):
                mm(ps[:, b, :], wkv[:, k, 128:256], xn[:, k, :], k == 0, k == 7)
            qknorm_rope(ps[:, b, :], pc(f"kg{l}"), KcT[:, c0:c0 + T], fillA)
            while fa:
                fillA()
            cur["p"] ^= 1

        ckpt(4)
        ensure("wo", l)
        load_xT(xsA, "xsA", 0, T)
        for t in range(NTILE):
            c0 = t * T
            xT = xTv()
            load_rope(t)
            if t == 0:
                rmsnorm(T, f"n1g{l}", xn)
            pre_b = (lambda c0=c0, p=1 - cur["p"]: load_xT(xsA, "xsA", c0 + T, T, p)) if t + 1 < NTILE else None
            wcache = {}
            wcache[0] = wload(4096, winb, l, "win", 0).rearrange("p (a b) -> p a b", b=512)
            wq = wload(4096, winb, l, "win", 2).rearrange("p (a b) -> p a b", b=512)

            def wget(blk):
                if blk not in wcache:
                    wcache[blk] = wload(4096, winb, l, "win", blk).rearrange("p (a b) -> p a b", b=512)
                return wcache[blk]

            def unit_u(j):
                w = wget(0)
                b = bank("L8")
                for k in range(8):
                    mm(ps[:, b, :], w[:, k, j * 128:(j + 1) * 128], xn[:, k, :], k == 0, k == 7)
                act(uT[:, j, :], ps[:, b, :], AF.Gelu_apprx_tanh)

            vslots = [vg[:, 0, :], vg[:, 1, :], tmpS[:, 0, :], tmpS[:, 1, :]]

            def unit_v(bq):
                w = wget(3)
                b = bank("L8")
                for k in range(8):
                    mm(ps[:, b, :], xn[:, k, bq * 128:(bq + 1) * 128], w[:, k, :], k == 0, k == 7)
                act(vslots[bq], ps[:, b, :], AF.Gelu_apprx_tanh)

            def ln_all():
                for bq in range(4):
                    v_ = vslots[bq]
                    so = bq * 8
                    P.op("dve", lambda e, v_=v_, so=so: e.bn_stats(out=stat[:, so:so + 6], in_=v_), reads=[v_], writes=[stat[:, so:so + 6]])
                    P.op("dve", lambda e, so=so: e.bn_aggr(out=stat[:, so + 6:so + 8], in_=stat[:, so:so + 6]),
                         reads=[stat[:, so:so + 6]], writes=[stat[:, so + 6:so + 8]])
                for bq in range(4):
                    so = bq * 8
                    act(stat[:, so + 7:so + 8], stat[:, so + 7:so + 8], AF.Sqrt, bias=pc("eps"))
                for bq in range(4):
                    so = bq * 8
                    recip(stat[:, so + 7:so + 8], stat[:, so + 7:so + 8])
                for bq in range(4):
                    v_ = vslots[bq]
                    so = bq * 8
                    ts("dve", v_, v_, stat[:, so + 6:so + 7], stat[:, so + 7:so + 8], ALU.subtract, ALU.mult)
                    tt("pool", v_, v_, lnG[:], ALU.mult)
                    tt("pool", vln[:, bq, :], v_, lnB[:], ALU.add)

            def unit_qb(j):
                w = wget(1)
                b = bank("L8")
                for k in range(8):
                    mm(ps[:, b, :], w[:, k, j * 128:(j + 1) * 128], xn[:, k, :], k == 0, k == 7)
                act(qbT[:, j, :], ps[:, b, :], AF.Identity, scale=0.125)

            for j in range(4):
                unit_u(j)
            for j in range(4):
                unit_v(j)
            ln_all()
            fb = []
            for j in range(4):
                fb.append(None)
                fb.append(lambda j=j: unit_qb(j))

            def fillB():
                if fb:
                    u_ = fb.pop(0)
                    if u_ is not None:
                        u_()

            for j in range(4):
                b = bank("L8")
                for k in range(8):
                    mm(ps[:, b, :], wq[:, k, j * 128:(j + 1) * 128], xn[:, k, :], k == 0, k == 7)
                qknorm_rope(ps[:, b, :], pc(f"qg{l}"), qcT[:, j, :], fillB)
            while fb:
                fillB()
            flush()
            if pre_b is not None:
                pre_b()
            if l + 1 < L:
                pump((CPL + NTILE - 1) // NTILE)
            ckpt(44)
            for j in range(4):
                b = bank("L8")
                for bq in range(4):
                    for hh in range(2):
                        g = 2 * j + hh
                        mm(ps[hh * 64:(hh + 1) * 64, b, bq * 128:(bq + 1) * 128], vln[:, bq, g * 64:(g + 1) * 64], wsT[:, g, :], True, True)
                tm = tmpS[:, j % 2, :]
                tt("dve", tm.rearrange("p (a b) -> p a b", b=128), ps[:, b, :].rearrange("p (a b) -> p a b", b=128),
                   bS[:, j:j + 1, :].to_broadcast([128, 4, 128]), ALU.add)
                tt("pool", oT[:, j, :], tm, uT[:, j, :], ALU.mult)
            ckpt(45)
            for qi in range(4):
                n = t * 4 + qi
                qs = slice(qi * 128, (qi + 1) * 128)
                chunks = [c for c in (0, 1, 2) if 0 <= n + c - 1 < NB]
                bO = [4, 5] if qi % 2 == 0 else [6, 7]
                nch = len(chunks)

                def SB(ci):
                    kb = n + chunks[ci] - 1
                    c = chunks[ci]
                    for g in range(2):
                        pr = slice(g * 64, (g + 1) * 64)
                        bk = ps[:, (ci % 2) * 2 + g, :]
                        mm(bk, KbT[pr, kb * 128:(kb + 1) * 128], qbT[pr, :, qs], True, False)
                        mm(bk, identb[:], biasHL[:, c * 2 + g, :], False, False)
                        mm(bk, identb[:], biasHL[:, 6 + c * 2 + g, :], False, True)

                def DB(ci):
                    pass

                def EB(ci):
                    c = chunks[ci]
                    edge = (n == HB - 1 and c == 2) or (n == HB and c == 0)
                    s2 = (ci % 2) * 2
                    act(PT[:, s2:s2 + 2, :].rearrange("p a b -> p (a b)"), ps[:, s2:s2 + 2, :].rearrange("p a b -> p (a b)"),
                        AF.Exp, bias=pc("cmask") if edge else pc("zero"))

                def PVB(ci):
                    kb = n + chunks[ci] - 1
                    for g in range(2):
                        mm(ps[:, bO[g], :], Vb[:, kb, g * 128:(g + 1) * 128], PT[:, (ci % 2) * 2 + g, :], ci == 0, ci == nch - 1)

                SB(0)
                if nch > 1:
                    SB(1)
                DB(0)
                for ci in range(nch):
                    EB(ci)
                    PVB(ci)
                    if ci + 2 < nch:
                        SB(ci + 2)
                    if ci + 1 < nch:
                        DB(ci + 1)
                for g in range(2):
                    r_ = rc[64:128, g, :]
                    tt("dve", r_, ps[64:128, bO[g], :], sinke[64:128, g, :], ALU.add)
                    act(r_, r_, AF.Ln)
                    act(r_, r_, AF.Exp, scale=-1.0)
                    tt("dve", oT[g * 64:(g + 1) * 64, 4:8, qs], ps[0:64, bO[g], :].rearrange("p (a b) -> p a b", b=128),
                       r_.rearrange("p (a b) -> p a b", b=128), ALU.mult)
            ckpt(46)
            for qi in range(4):
                n = t * 4 + qi
                qs = slice(qi * 128, (qi + 1) * 128)
                bO = [4, 5] if qi % 2 == 0 else [6, 7]

                def SC(kb):
                    for g in range(2):
                        pr = slice(g * 64, (g + 1) * 64)
                        mm(ps[:, (kb % 2) * 2 + g, :], KcT[pr, kb * 128:(kb + 1) * 128], qcT[pr, :, qs], True, True)

                def EC(kb):
                    cross = (n < HB) != (kb < HB)
                    s2 = (kb % 2) * 2
                    act(PT[:, s2:s2 + 2, :].rearrange("p a b -> p (a b)"), ps[:, s2:s2 + 2, :].rearrange("p a b -> p (a b)"),
                        AF.Exp, bias=pc("cmask") if cross else pc("zero"), scale=0.125)

                def PVC(kb):
                    for g in range(2):
                        mm(ps[:, bO[g], :], Vc[:, kb, g * 128:(g + 1) * 128], PT[:, (kb % 2) * 2 + g, :], kb == 0, kb == NB - 1)

                SC(0)
                SC(1)
                for kb in range(NB):
                    EC(kb)
                    PVC(kb)
                    if kb + 2 < NB:
                        SC(kb + 2)
                for g in range(2):
                    r_ = rc[64:128, g, :]
                    recip(r_, ps[64:128, bO[g], :])
                    tt("dve", oT[g * 64:(g + 1) * 64, 8:12, qs], ps[0:64, bO[g], :].rearrange("p (a b) -> p a b", b=128),
                       r_.rearrange("p (a b) -> p a b", b=128), ALU.mult)
            if pre_b is not None:
                norm_sq(1 - cur["p"], T, sq8b)
            ckpt(47)
            for kq in range(2):
                for nbr in range(3):
                    wg_ = wload(4096, wgb, l, "wg", nbr * 2 + kq).rearrange("p (a b) -> p a b", b=512)
                    wb_ = wload(2048, wbrb, l, "wbr", nbr * 2 + kq).rearrange("p (a b) -> p a b", b=512)
                    for kk in range(4):
                        k = kq * 4 + kk
                        cs = slice(kk * 128, (kk + 1) * 128)
                        bY = bank("L8")
                        for j in range(4):
                            mm(ps[:, bY, :], wb_[:, j, cs], oT[:, nbr * 4 + j, :], j == 0, j == 3)
                        bG = bank("L8")
                        for i in range(8):
                            mm(ps[:, bG, :], wg_[:, i, cs], xn[:, i, :], i == 0, i == 7)
                        g_ = gt[:, kk % 2, :]
                        act(g_, ps[:, bG, :], AF.Sigmoid, bias=pc(f"bg{l}", nbr * 8 + k))
                        if nbr == 0:
                            tt("dve", acc[:, kk, :], g_, ps[:, bY, :], ALU.mult)
                        elif nbr == 1:
                            tt("dve", g_, g_, ps[:, bY, :], ALU.mult)
                            tt("pool", acc[:, kk, :], acc[:, kk, :], g_, ALU.add)
                        else:
                            tt("dve", g_, g_, ps[:, bY, :], ALU.mult)
                            tt("pool", mg[:, k, :], acc[:, kk, :], g_, ALU.add)
            if pre_b is not None:
                norm_fin(1 - cur["p"], T, f"n1g{l}", xn, sq8b)
            ckpt(48)
            for kq in range(2):
                w = wload(4096, wob, l, "wo", kq).rearrange("p (a b) -> p a b", b=512)
                for kk in range(4):
                    k = kq * 4 + kk
                    b = bank("L8")
                    for i in range(8):
                        mm(ps[:, b, :], w[:, i, kk * 128:(kk + 1) * 128], mg[:, i, :], i == 0, i == 7)
                    tt("dve", xT[:, k, :], ps[:, b, :], xT[:, k, :], ALU.add)
            deferred.append(lambda c0=c0, xT=xT, p=cur["p"]: dma("sp", xsB[:, :, c0:c0 + T], xT[:, :, :], ("xTst", p), reads=[xT[:, :, :]],
                                                                 writes=xs_keys("xsB", c0, c0 + T)))
            cur["p"] ^= 1
        flush()

        ckpt(5)
        ensure("wdn", l)
        tiles = []
        for hbase in (0, H):
            a = hbase
            while a < hbase + H:
                bnd = min(a + FT, hbase + H)
                tiles.append((a, bnd))
                a = bnd
        def cgeom(a, bnd):
            lo = a - 1 if a > 0 else a
            hi = bnd + 1 if bnd < NT else bnd
            return lo, hi

        lo_, hi_ = cgeom(*tiles[0])
        load_xT(xsB, "xsB", lo_, hi_ - lo_)
        for ti, (a, bnd) in enumerate(tiles):
            lo, hi = cgeom(a, bnd)
            ncols = hi - lo
            oi = a - lo
            wo_ = bnd - a
            x0 = 0 if a > 0 else 1
            y0 = 0 if bnd < NT else 1
            xT = xTv()
            if ti == 0:
                rmsnorm(ncols, f"n2g{l}", xn)
            pre_c = None
            ncn = 0
            if ti + 1 < len(tiles):
                lo_, hi_ = cgeom(*tiles[ti + 1])
                ncn = hi_ - lo_
                pre_c = lambda lo_=lo_, hi_=hi_, p=1 - cur["p"]: load_xT(xsB, "xsB", lo_, hi_ - lo_, p)
            for c in range(NFC):
                wu = wload(2048, wupb, l, "wup", c).rearrange("p (a b) -> p a b", b=256)
                if c == 2:
                    flush()
                if c == 8 and pre_c is not None:
                    pre_c()
                if c == 14 and pre_c is not None:
                    norm_sq(1 - cur["p"], ncn)
                hb_ = []
                for gv in range(2):
                    b = bank("L8")
                    for i in range(8):
                        mm(ps[:, b, 0:ncols], wu[:, i, gv * 128:(gv + 1) * 128], xn[:, i, 0:ncols], i == 0, i == 7)
                    if a == H and a > 0:
                        ts("dve", ps[:, b, 0:1], ps[:, b, 0:1], pc("m01"), None, ALU.mult)
                    if bnd == H:
                        ts("dve", ps[:, b, ncols - 1:ncols], ps[:, b, ncols - 1:ncols], pc("m01"), None, ALU.mult)
                    ch = gv * NFC + c
                    h_ = hc[:, (c % 2) * 2 + gv, :]
                    act(h_[:, 0:wo_], ps[:, b, oi:oi + wo_], AF.Identity, bias=pc(f"cb{l}", ch), scale=pc(f"cw1{l}", ch))
                    if wo_ - x0 > 0:
                        stt(h_[:, x0:wo_], ps[:, b, oi + x0 - 1:oi + wo_ - 1], pc(f"cw0{l}", ch), h_[:, x0:wo_], ALU.mult, ALU.add)
                    if wo_ - y0 > 0:
                        stt(h_[:, 0:wo_ - y0], ps[:, b, oi + 1:oi + 1 + wo_ - y0], pc(f"cw2{l}", ch), h_[:, 0:wo_ - y0], ALU.mult, ALU.add)
                    hb_.append(h_)
                s_ = sg[:, c % 2, :]
                act(s_[:, 0:wo_], hb_[0][:, 0:wo_], AF.Silu)
                tt("pool", actb[:, c, 0:wo_], s_[:, 0:wo_], hb_[1][:, 0:wo_], ALU.mult)
            if pre_c is not None:
                norm_fin(1 - cur["p"], ncn, f"n2g{l}", xn)
            for k in range(8):
                wd = wload(NFC * 128, wdnb, l, "wdn", k).rearrange("p (a b) -> p a b", b=128)
                b = bank("L8")
                for c in range(NFC):
                    mm(ps[:, b, 0:wo_], wd[:, c, :], actb[:, c, 0:wo_], c == 0, c == NFC - 1)
                tt("dve", xT[:, k, oi:oi + wo_], ps[:, b, 0:wo_], xT[:, k, oi:oi + wo_], ALU.add)
            deferred.append(lambda a=a, bnd=bnd, xT=xT, oi=oi, wo_=wo_, p=cur["p"]: dma(
                "sp", xsA[:, :, a:bnd], xT[:, :, oi:oi + wo_], ("xTst", p), reads=[xT[:, :, oi:oi + wo_]], writes=xs_keys("xsA", a, bnd)))
            cur["p"] ^= 1
        flush()

    ckpt(6)
    load_xT(xsA, "xsA", 0, T)
    for t in range(NTILE):
        c0 = t * T
        rmsnorm(T, "fing", yT)
        if t + 1 < NTILE:
            load_xT(xsA, "xsA", c0 + T, T, 1 - cur["p"])
        cur["p"] ^= 1
        for bq in range(4):
            yo_ = yo[:, bq % 2, :]
            for hf in range(2):
                b = bank("L8")
                for kk in range(4):
                    k = hf * 4 + kk
                    tr(ps[:, b, kk * 128:(kk + 1) * 128], yT[:, k, bq * 128:(bq + 1) * 128])
                cp("dve" if hf == 0 else "act", yo_[:, hf * 512:(hf + 1) * 512], ps[:, b, :])
            dma("sp", y_d[c0 + bq * 128:c0 + (bq + 1) * 128, :], yo_, ("yst", bq % 2), reads=[yo_], writes=[("y", t, bq)])

    counts = P.finalize()
    CH = P.CH
    sem_stack = ExitStack()
    esem = {}
    for e in P.ENGS:
        n = (counts[e] + CH - 1) // CH
        esem[e] = [sem_stack.enter_context(nc.semaphore(f"s_{e}_{i}")) for i in range(max(n, 1))]
    dsem = {}
    for k in P.dma_cum:
        dsem[k] = sem_stack.enter_context(nc.semaphore(f"d{len(dsem)}"))

    def emit_engine(ename, eng):
        waited_c = {e: 0 for e in P.ENGS}
        waited_d = {}
        for o in P.ops[ename]:
            need = {}
            for d in o.deps:
                if d.count > need.get(d.eng, 0):
                    need[d.eng] = d.count
            for se, cnt in need.items():
                if cnt > waited_c[se]:
                    eng.wait_ge(esem[se][(cnt - 1) // CH], (cnt - 1) % CH + 1)
                    waited_c[se] = cnt
            for sk, val in o.dma_waits.items():
                if val > waited_d.get(sk, 0):
                    eng.wait_ge(dsem[sk], val)
                    waited_d[sk] = val
            if o.fn is None:
                continue
            ins = o.fn(eng)
            if o.is_dma:
                ins.then_inc(dsem[o.semkey], 16)
            elif o.signal:
                ins.then_inc(esem[ename][(o.count - 1) // CH], 1)
        if ename == "sp":
            for sk, val in P.dma_cum.items():
                eng.wait_ge(dsem[sk], val)

    with nc.Block() as block:
        @block.tensor
        def _(e):
            emit_engine("pe", e)

        @block.scalar
        def _(e):
            emit_engine("act", e)

        @block.vector
        def _(e):
            emit_engine("dve", e)

        @block.gpsimd
        def _(e):
            emit_engine("pool", e)

        @block.sync
        def _(e):
            emit_engine("sp", e)

    sem_stack.close()
    es.close()
    return nc


_PROG_CACHE = {}


def run_cores(core_tokens, two_seq_flags, params, NT, L):
    key = (NT, L)
    if key not in _PROG_CACHE:
        _PROG_CACHE[key] = build_program(NT, L)
    nc = _PROG_CACHE[key]
    shared = _shared_inputs(params, L)
    rope2 = _rope_host(NT, True)
    rope1 = _rope_host(NT, False)
    in_maps = []
    for xt, two in zip(core_tokens, two_seq_flags):
        m = dict(shared)
        m["x"] = np.ascontiguousarray(xt, dtype=np.float32)
        m["pcol"] = _pcol_host(params, L, two)
        m["rope"] = rope2 if two else rope1
        in_maps.append(m)
    res = run_bass_kernel_spmd(nc, in_maps, core_ids=list(range(len(in_maps))))
    return [np.asarray(r["y"]) for r in res.results]


def kernel(x_prompt, x_sample, rel_bias, norm1_g, w_in, ln_v_g, ln_v_b, w_spatial, b_spatial, sink,
           q_norm_g, k_norm_g, w_gate, b_gate, w_branch, w_out, norm2_g, w_up, conv_w, conv_b, w_down, final_g):
    params = dict(rel_bias=rel_bias, norm1_g=norm1_g, w_in=w_in, ln_v_g=ln_v_g, ln_v_b=ln_v_b, w_spatial=w_spatial,
                  b_spatial=b_spatial, sink=sink, q_norm_g=q_norm_g, k_norm_g=k_norm_g, w_gate=w_gate, b_gate=b_gate,
                  w_branch=w_branch, w_out=w_out, norm2_g=norm2_g, w_up=w_up, conv_w=conv_w, conv_b=conv_b,
                  w_down=w_down, final_g=final_g)
    params = {k: np.asarray(v, dtype=np.float32) for k, v in params.items()}
    xp = np.asarray(x_prompt, dtype=np.float32)
    xs = np.asarray(x_sample, dtype=np.float32)
    NT = 4096
    toks = [xp[2 * c:2 * c + 2].reshape(NT, D) for c in range(4)] + [xs[c] for c in range(4)]
    flags = [True] * 4 + [False] * 4
    ys = run_cores(toks, flags, params, NT, 2)
    y_prompt = np.stack(ys[0:4]).reshape(8, 2048, D).astype(np.float32)
    y_sample = np.stack(ys[4:8]).reshape(4, 4096, D).astype(np.float32)
    return (y_prompt, y_sample)
```

```python
import math
from contextlib import ExitStack

import numpy as np
import concourse.bass as bass
import concourse.mybir as mybir
from concourse.bass_utils import run_bass_kernel_spmd

F32 = mybir.dt.float32
BF16 = mybir.dt.bfloat16
AF = mybir.ActivationFunctionType
ALU = mybir.AluOpType

D = 1024
KD = 8
HD = 64
DIN = 2560
DG = 3072
DFF = 2816
NFC = 22
EPS = 1e-6
NEG = -30000.0
T = 512
FT = 410
NS = 3


class _Op:
    __slots__ = ("eng", "fn", "idx", "deps", "dma_waits", "is_dma", "semkey", "cum", "signal", "count")


class Prog:
    ENGS = ("pe", "act", "dve", "pool", "sp")
    GRAN = 512
    CH = 6000

    def __init__(self):
        self.ops = {e: [] for e in self.ENGS}
        self.res = {}
        self.dma_cum = {}

    def _keys(self, a):
        if not isinstance(a, bass.AP):
            return [a]
        sp = str(a.space)
        name = a.tensor.name
        es = mybir.dt.size(a.dtype)
        dims = a.ap
        pstep = dims[0][0]
        if sp == "PSUM":
            gran = 2048
        else:
            gran = self.GRAN
        foff = (a.offset % pstep) if pstep > 0 else a.offset
        starts = [foff]
        for (st, cnt) in dims[1:-1]:
            starts = [s + i * st for s in starts for i in range(cnt)]
        lst, lcnt = dims[-1] if len(dims) > 1 else (1, 1)
        ln = (abs(lst) * (lcnt - 1) + 1)
        ks = set()
        for s in starts:
            b0 = (s * es) // gran
            b1 = ((s + ln) * es - 1) // gran
            for b in range(b0, b1 + 1):
                ks.add((name, b))
        return list(ks)

    def op(self, eng, fn, reads=(), writes=(), dma=None):
        if getattr(self, "frozen", False):
            return None
        o = _Op()
        o.eng = eng
        o.fn = fn
        o.idx = len(self.ops[eng])
        o.deps = set()
        o.dma_waits = {}
        o.is_dma = dma is not None
        o.semkey = dma
        o.signal = False
        o.count = 0
        rk = [k for r in reads for k in self._keys(r)]
        wk = [k for w in writes for k in self._keys(w)]
        for k in rk:
            st = self.res.get(k)
            if st is not None and st[0] is not None:
                o.deps.add(st[0])
        for k in wk:
            st = self.res.get(k)
            if st is not None:
                if st[0] is not None:
                    o.deps.add(st[0])
                for r in st[1]:
                    o.deps.add(r)
        o.deps.discard(o)
        for d in o.deps:
            if d.is_dma:
                o.dma_waits[d.semkey] = self.dma_cum[d.semkey]
        if o.is_dma:
            self.dma_cum[dma] = self.dma_cum.get(dma, 0) + 16
            o.cum = self.dma_cum[dma]
        for k in rk:
            st = self.res.setdefault(k, [None, []])
            if not o.is_dma:
                st[1] = [r for r in st[1] if r.is_dma or r.eng != eng]
            st[1].append(o)
        for k in wk:
            self.res[k] = [o, []]
        self.ops[eng].append(o)
        return o

    def finalize(self):
        for e in self.ENGS:
            for o in self.ops[e]:
                keep = []
                for d in o.deps:
                    if d.is_dma:
                        continue
                    if d.eng == o.eng and not o.is_dma:
                        if o.eng == "pe":
                            continue
                        if o.idx - d.idx > 3:
                            continue
                    d.signal = True
                    keep.append(d)
                o.deps = keep
        for e in self.ENGS:
            c = 0
            for o in self.ops[e]:
                if o.signal:
                    c += 1
                    o.count = c
        return {e: max([o.count for o in self.ops[e]] + [0]) for e in self.ENGS}


def _pkn(w, kc):
    n = w.shape[1]
    return np.ascontiguousarray(w.reshape(kc, 128, n).transpose(1, 0, 2))


def _t5_bucket_np(rel):
    half = 16
    max_exact = 8
    ret = np.where(rel > 0, half, 0)
    n = np.abs(rel)
    nf = np.maximum(n, 1).astype(np.float32)
    large = max_exact + (np.log(nf / np.float32(max_exact)) / np.float32(math.log(128 / max_exact))
                         * np.float32(half - max_exact)).astype(np.int32)
    large = np.minimum(large, half - 1)
    return ret + np.where(n < max_exact, n, large)


_QPERM = np.concatenate([np.concatenate([j * 64 + np.arange(64), (4 + j) * 64 + np.arange(64)]) for j in range(4)])


def _pcol_layout(L):
    off = {}
    c = 0

    def add(name, n):
        nonlocal c
        off[name] = c
        c += n

    add("zero", 1)
    add("eps", 1)
    add("cmask", 1)
    add("m01", 1)
    for l in range(L):
        add(f"n1g{l}", 8)
        add(f"n2g{l}", 8)
        add(f"bg{l}", 24)
        add(f"cw0{l}", 44)
        add(f"cw1{l}", 44)
        add(f"cw2{l}", 44)
        add(f"cb{l}", 44)
        add(f"qg{l}", 1)
        add(f"kg{l}", 1)
    add("fing", 8)
    return off, c


def _shared_inputs(p, L):
    f = np.float32
    out = {}
    wl = {k: [] for k in ("w_in", "w_gate", "w_br", "w_out", "w_up", "w_down")}
    for l in range(L):
        wi = p["w_in"][l]
        cat = np.concatenate([wi[:, 0:512], wi[:, 1024:1536][:, _QPERM], wi[:, 1792:2304][:, _QPERM],
                              wi[:, 512:1024], wi[:, 1536:1664], wi[:, 2304:2432], wi[:, 1664:1792], wi[:, 2432:2560]], axis=1)
        wl["w_in"].append(_pkn(cat, 8).reshape(128, 8, 5, 512).transpose(0, 2, 1, 3).reshape(128, 5, 4096))
        wl["w_gate"].append(_pkn(p["w_gate"][l], 8).reshape(128, 8, 6, 512).transpose(0, 2, 1, 3).reshape(128, 6, 4096))
        rows = np.concatenate([p["w_branch"][l][0], p["w_branch"][l][1][_QPERM], p["w_branch"][l][2][_QPERM]], axis=0)
        wl["w_br"].append(_pkn(rows, 12).reshape(128, 3, 4, 2, 512).transpose(0, 1, 3, 2, 4).reshape(128, 6, 2048))
        wl["w_out"].append(_pkn(p["w_out"][l], 8).reshape(128, 8, 2, 512).transpose(0, 2, 1, 3).reshape(128, 2, 4096))
        up = _pkn(p["w_up"][l], 8)
        gv = np.stack([up[:, :, 0:DFF].reshape(128, 8, NFC, 128), up[:, :, DFF:].reshape(128, 8, NFC, 128)], axis=3)
        wl["w_up"].append(gv.transpose(0, 2, 1, 3, 4).reshape(128, NFC, 2048))
        wl["w_down"].append(_pkn(p["w_down"][l], NFC).reshape(128, NFC, 8, 128).transpose(0, 2, 1, 3).reshape(128, 8, NFC * 128))
    for k in wl:
        out[k] = np.ascontiguousarray(np.stack(wl[k])).astype(f)
    out["lngb"] = np.stack([np.stack([np.broadcast_to(p["ln_v_g"][l], (128, 512)),
                                      np.broadcast_to(p["ln_v_b"][l], (128, 512))]) for l in range(L)]).astype(f)
    bs = np.zeros((L, 128, 4, 128), f)
    for l in range(L):
        for j in range(4):
            bs[l, 0:64, j, :] = p["b_spatial"][l][2 * j][None, :]
            bs[l, 64:128, j, :] = p["b_spatial"][l][2 * j + 1][None, :]
    out["bS"] = bs
    out["wsT"] = np.ascontiguousarray(np.stack([p["w_spatial"][l].transpose(2, 0, 1) for l in range(L)])).astype(f)
    sk = np.zeros((L, 128, 2, 512), f)
    for l in range(L):
        for g in range(2):
            for hl in range(4):
                sk[l, :, g, hl * 128:(hl + 1) * 128] = p["sink"][l][g * 4 + hl]
    out["sinkb"] = sk
    out["ident"] = np.eye(128, dtype=f)
    out["onesm"] = np.full((128, 128), 1.0 / 1024.0, f)
    blk = np.zeros((128, 128), f)
    blk[0:64, 0:64] = 1.0 / 64.0
    blk[64:128, 64:128] = 1.0 / 64.0
    out["blk"] = blk
    R = np.zeros((128, 128), f)
    for base in (0, 32, 64, 96):
        for e in range(16):
            R[base + e + 16, base + e] = -1.0
            R[base + e, base + e + 16] = 1.0
    out["rotm"] = R
    oh = np.zeros((33, 3, 256), f)
    for c in range(3):
        s = np.arange(255)
        rel = 128 * (c - 1) + 127 - s
        b = _t5_bucket_np(rel.astype(np.int32))
        oh[b, c, s] = 1.0
        oh[32, c, s] = np.where(np.abs(rel) <= 128, 0.0, NEG)
    out["ohm"] = oh
    rr = np.ones((33, 8, 128), f)
    rr[0:32] = p["rel_bias"][:, :, None]
    out["relrep"] = rr
    return out


def _pcol_host(p, L, two_seq):
    off, n = _pcol_layout(L)
    pc = np.zeros((128, n), np.float32)
    pc[:, off["eps"]] = EPS
    pc[:, off["cmask"]] = NEG if two_seq else 0.0
    pc[:, off["m01"]] = 0.0 if two_seq else 1.0

    def col8(v):
        return v.reshape(-1, 128).T

    for l in range(L):
        pc[:, off[f"n1g{l}"]:off[f"n1g{l}"] + 8] = col8(p["norm1_g"][l])
        pc[:, off[f"n2g{l}"]:off[f"n2g{l}"] + 8] = col8(p["norm2_g"][l])
        pc[:, off[f"bg{l}"]:off[f"bg{l}"] + 24] = col8(p["b_gate"][l])
        for j in range(3):
            pc[:, off[f"cw{j}{l}"]:off[f"cw{j}{l}"] + 44] = col8(p["conv_w"][l][j])
        pc[:, off[f"cb{l}"]:off[f"cb{l}"] + 44] = col8(p["conv_b"][l])
        pc[:, off[f"qg{l}"]] = np.tile(p["q_norm_g"][l], 2)
        pc[:, off[f"kg{l}"]] = np.tile(p["k_norm_g"][l], 2)
    pc[:, off["fing"]:off["fing"] + 8] = col8(p["final_g"])
    return pc


def _rope_host(NT, two_seq):
    pos = np.arange(NT)
    if two_seq:
        pos = pos % (NT // 2)
    row = (pos // 64).astype(np.float32)
    col = (pos % 64).astype(np.float32)
    inv = (np.float32(10000.0) ** (-np.arange(16, dtype=np.float32) / np.float32(16))).astype(np.float32)
    tab = np.zeros((2, 128, NT), np.float32)
    for pp in range(128):
        d = pp % 64
        ax = row if d < 32 else col
        i = (d % 32) % 16
        ang = (ax * inv[i]).astype(np.float32)
        tab[0, pp] = np.cos(ang)
        tab[1, pp] = np.sin(ang)
    return tab


def build_program(NT, L):
    NB = NT // 128
    NTILE = NT // T
    H = NT // 2
    HB = NB // 2
    off, NPC = _pcol_layout(L)

    nc = bass.Bass("TRN2", target_bir_lowering=False)
    P = Prog()
    import os
    STOP = int(os.environ.get("KSTOP", "99"))

    def ckpt(n):
        if STOP == n:
            P.frozen = True

    def din(name, shape, dt=F32):
        return nc.dram_tensor(name, list(shape), dt, kind="ExternalInput")

    x_d = din("x", [NT, D])
    win_d = din("w_in", [L, 128, 5, 4096])
    wg_d = din("w_gate", [L, 128, 6, 4096])
    wbr_d = din("w_br", [L, 128, 6, 2048])
    wo_d = din("w_out", [L, 128, 2, 4096])
    wup_d = din("w_up", [L, 128, NFC, 2048])
    wdn_d = din("w_down", [L, 128, 8, NFC * 128])
    lngb_d = din("lngb", [L, 2, 128, 512])
    bS_d = din("bS", [L, 128, 4, 128])
    wsT_d = din("wsT", [L, 128, 8, 128])
    sinkb_d = din("sinkb", [L, 128, 2, 512])
    ident_d = din("ident", [128, 128])
    onesm_d = din("onesm", [128, 128])
    blk_d = din("blk", [128, 128])
    rotm_d = din("rotm", [128, 128])
    ohm_d = din("ohm", [33, 3, 256])
    relrep_d = din("relrep", [33, 8, 128])
    pcol_d = din("pcol", [128, NPC])
    rope_d = din("rope", [2, 128, NT])
    y_d = nc.dram_tensor("y", [NT, D], F32, kind="ExternalOutput")

    def dint(name, shape, dt):
        return nc.dram_tensor(name, list(shape), dt, kind="Internal")

    winb = dint("winb", [L, 128, 5, 4096], BF16)
    wgb = dint("wgb", [L, 128, 6, 4096], BF16)
    wbrb = dint("wbrb", [L, 128, 6, 2048], BF16)
    wob = dint("wob", [L, 128, 2, 4096], BF16)
    wupb = dint("wupb", [L, 128, NFC, 2048], BF16)
    wdnb = dint("wdnb", [L, 128, 8, NFC * 128], BF16)
    xsA = dint("xsA", [128, 8, NT], F32)
    xsB = dint("xsB", [128, 8, NT], F32)
    tabx = dint("tabx", [3, 128, 8, 256], F32)

    es = ExitStack()

    def sb(name, shape, dt):
        return es.enter_context(nc.sbuf_tensor("sb_" + name, list(shape), dt))

    KbT = sb("KbT", [128, NT], BF16)
    KcT = sb("KcT", [128, NT], BF16)
    Vb = sb("Vb", [128, NB, 256], BF16)
    Vc = sb("Vc", [128, NB, 256], BF16)
    biasHL = sb("biasHL", [128, 12, 512], BF16)
    identb = sb("identb", [128, 128], BF16)
    pcol = sb("pcol", [128, NPC], F32)
    ident = sb("ident", [128, 128], F32)
    onesm = sb("onesm", [128, 128], BF16)
    blkb = sb("blkb", [128, 128], BF16)
    rotb = sb("rotb", [128, 128], BF16)
    lnG = sb("lnG", [128, 512], F32)
    lnB = sb("lnB", [128, 512], F32)
    bS = sb("bS", [128, 4, 128], F32)
    wsT = sb("wsT", [128, 8, 128], BF16)
    sinke = sb("sinke", [128, 2, 512], F32)
    ropeC = sb("ropeC", [128, 512], F32)
    ropeS = sb("ropeS", [128, 512], F32)
    xTd = sb("xTd", [128, 2, 4096], F32)
    cur = {"p": 0}

    def xTv(p=None):
        p = cur["p"] if p is None else p
        return xTd[:, p, :].rearrange("q (a b) -> q a b", b=512)
    xn = sb("xn", [128, 8, 512], BF16)
    sq = sb("sq", [128, 2, 512], BF16)
    rstd = sb("rstd", [128, 512], F32)
    wbuf = sb("wbuf", [128, NS * 4096], BF16)
    stat = sb("stat", [128, 32], F32)
    ARENA = 58 * 1024 // 2
    arena = sb("arena", [128, ARENA], BF16)
    ps = es.enter_context(nc.psum_tensor("ps", [128, 8, 512], F32))

    def av(byte_off, nbytes, dt, shape3=None):
        a = arena[:, byte_off // 2:(byte_off + nbytes) // 2]
        if dt == F32:
            a = a.bitcast(F32)
        if shape3 is not None:
            a = a.rearrange("p (a b) -> p a b", b=shape3)
        return a

    K = 1024
    uT = av(0, 8 * K, F32, 512)
    vg = av(8 * K, 4 * K, F32, 512)
    vln = av(12 * K, 4 * K, BF16, 512)
    rt = av(16 * K, 8 * K, F32, 512)
    knb = av(24 * K, 1 * K, BF16)
    sqc = av(25 * K, 1 * K, BF16)
    mg = av(0, 8 * K, BF16, 512)
    gt = av(8 * K, 4 * K, F32, 512)
    acc = av(12 * K, 8 * K, F32, 512)
    qbT = av(26 * K, 4 * K, BF16, 512)
    qcT = av(30 * K, 4 * K, BF16, 512)
    oT = av(34 * K, 12 * K, BF16, 512)
    tmpS = av(46 * K, 4 * K, F32, 512)
    PT = av(50 * K, 4 * K, BF16, 512)
    rc = av(54 * K, 4 * K, F32, 512)
    actb = av(0, 22 * K, BF16, 512)
    hc = av(22 * K, 8 * K, F32, 512)
    sg = av(30 * K, 4 * K, F32, 512)
    yT = av(0, 16 * K, F32, 512)
    yo = av(16 * K, 8 * K, F32, 1024)
    xin = av(26 * K, 16 * K, F32, 1024)
    trep = av(0, 8 * K, F32, 256)
    biasT = av(42 * K, 12 * K, F32, 512)
    bh32 = av(54 * K, 4 * K, F32, 512)
    relrep_s = av(8 * K, 4 * K, F32, 128)
    ohm_s = av(12 * K, 3 * K, F32, 256)

    def pc(name, i=0):
        c = off[name] + i
        return pcol[:, c:c + 1]

    rot_state = {"L8": 0, "S": 0, "O": 0, "L": 0}

    def bank(role):
        sets = {"L8": [0, 1, 2, 3, 4, 5, 6, 7], "S": [0, 1, 2, 3], "O": [4, 5], "L": [6, 7]}[role]
        b = sets[rot_state[role] % len(sets)]
        rot_state[role] += 1
        return b

    def dma(q, out, in_, semkey, reads=(), writes=()):
        return P.op(q, lambda e, out=out, in_=in_: e.dma_start(out=out, in_=in_), reads=reads, writes=writes, dma=semkey)

    def mm(out, lhsT, rhs, start, stop):
        return P.op("pe", lambda e: e.matmul(out, lhsT, rhs, start=start, stop=stop), reads=[lhsT, rhs], writes=[out])

    def tr(out, in_):
        return P.op("pe", lambda e: e.transpose(out, in_, ident[:]), reads=[in_, ident[:]], writes=[out])

    def act(out, in_, func, bias=None, scale=1.0, extra_reads=()):
        rd = [in_] + list(extra_reads)
        if isinstance(bias, bass.AP):
            rd.append(bias)
        if isinstance(scale, bass.AP):
            rd.append(scale)
        kw = {}
        if bias is not None:
            kw["bias"] = bias
        return P.op("act", lambda e: e.activation(out=out, in_=in_, func=func, scale=scale, **kw), reads=rd, writes=[out])

    def tt(eng, out, in0, in1, op):
        return P.op(eng, lambda e: e.tensor_tensor(out=out, in0=in0, in1=in1, op=op), reads=[in0, in1], writes=[out])

    def ts(eng, out, in0, s1, s2, op0, op1=None):
        rd = [in0] + [s for s in (s1, s2) if isinstance(s, bass.AP)]
        if op1 is None:
            return P.op(eng, lambda e: e.tensor_scalar(out=out, in0=in0, scalar1=s1, scalar2=None, op0=op0), reads=rd, writes=[out])
        return P.op(eng, lambda e: e.tensor_scalar(out=out, in0=in0, scalar1=s1, scalar2=s2, op0=op0, op1=op1), reads=rd, writes=[out])

    def stt(out, in0, scalar, in1, op0, op1):
        rd = [in0, in1] + ([scalar] if isinstance(scalar, bass.AP) else [])
        return P.op("dve", lambda e: e.scalar_tensor_tensor(out=out, in0=in0, scalar=scalar, in1=in1, op0=op0, op1=op1), reads=rd, writes=[out])

    def cp(eng, out, in_):
        if eng == "act":
            return act(out, in_, AF.Copy)
        return P.op(eng, lambda e: e.tensor_copy(out=out, in_=in_), reads=[in_], writes=[out])

    def recip(out, in_):
        return P.op("dve", lambda e: e.reciprocal(out=out, in_=in_), reads=[in_], writes=[out])

    wstate = {"h": 0}
    NH = NS * 2

    def walloc(nel):
        nh = (nel + 2047) // 2048
        if wstate["h"] + nh > NH:
            wstate["h"] = 0
        h = wstate["h"]
        wstate["h"] = (h + nh) % NH
        return h

    def wload(nel, dstb, l, nm, blk):
        h = walloc(nel)
        dst_ap = wbuf[:, h * 2048:h * 2048 + nel]
        dma("sp", dst_ap, dstb[l, :, blk, :], ("w", h), reads=[("wbf", nm, l, blk)], writes=[dst_ap])
        return dst_ap

    conv_chunks = []
    for l in range(L):
        conv_chunks += [(win_d, winb, l, "win", 4, 5), (win_d, winb, l, "win", 0, 2), (win_d, winb, l, "win", 2, 4),
                        (wg_d, wgb, l, "wg", 0, 3), (wg_d, wgb, l, "wg", 3, 6), (wbr_d, wbrb, l, "wbr", 0, 6),
                        (wo_d, wob, l, "wo", 0, 2),
                        (wup_d, wupb, l, "wup", 0, 6), (wup_d, wupb, l, "wup", 6, 12), (wup_d, wupb, l, "wup", 12, 18),
                        (wup_d, wupb, l, "wup", 18, 22), (wdn_d, wdnb, l, "wdn", 0, 4), (wdn_d, wdnb, l, "wdn", 4, 8)]
    cstate = {"n": 0}

    def pump(n):
        for _ in range(n):
            i = cstate["n"]
            if i >= len(conv_chunks):
                return
            src, dst, l, nm, k0, k1 = conv_chunks[i]
            rd = [("cvdone", i - 2)] if i >= 2 else []
            dma("pool", dst[l, :, k0:k1, :], src[l, :, k0:k1, :], ("cv", i), reads=rd,
                writes=[("wbf", nm, l, k) for k in range(k0, k1)] + [("cvdone", i)])
            cstate["n"] += 1

    def ensure(nm, l):
        last = max(i for i, c in enumerate(conv_chunks) if c[3] == nm and c[2] == l)
        while cstate["n"] <= last:
            pump(1)

    CPL = 13

    dma("sp", pcol[:], pcol_d[:, :], "c_pcol", writes=[pcol[:]])
    dma("sp", ident[:], ident_d[:, :], "c_ident", writes=[ident[:]])
    dma("pool", onesm[:], onesm_d[:, :], "c_ones", writes=[onesm[:]])
    dma("pool", blkb[:], blk_d[:, :], "c_blk", writes=[blkb[:]])
    dma("pool", rotb[:], rotm_d[:, :], "c_rot", writes=[rotb[:]])
    ckpt(0)
    ensure("win", 0)
    ckpt(1)
    def setup_bias():
        dma("sp", relrep_s[0:33, :, :], relrep_d[:, :, :], "c_rel", writes=[relrep_s[0:33, :, :]])
        dma("sp", ohm_s[0:33, :, :], ohm_d[:, :, :], "c_ohm", writes=[ohm_s[0:33, :, :]])
        for c in range(3):
            for h in range(8):
                b = bank("L8")
                mm(ps[:, b, 0:256], relrep_s[0:33, h, :], ohm_s[0:33, c, :], True, True)
                cp("dve" if h % 2 == 0 else "act", trep[:, h, :], ps[:, b, 0:256])
            dma("sp", tabx[c, :, :, :], trep[:, :, :], "tabx", reads=[trep[:, :, :]], writes=[("tabx", c)])
            for g in range(2):
                src = bass.AP(tabx, c * 128 * 2048 + g * 4 * 256 + 127, [[2047, 128], [256, 4], [1, 128]])
                dst = biasT[:, c * 2 + g, :].rearrange("p (a b) -> p a b", b=128)
                dma("sp", dst, src, "biasT", reads=[("tabx", c)], writes=[biasT[:, c * 2 + g, :]])
        cp("dve", identb[:], ident[:])
        for i in range(6):
            cp("act", biasHL[:, i, :], biasT[:, i, :])
            cp("dve", bh32[:, 0, :], biasHL[:, i, :])
            tt("dve", bh32[:, 1, :], biasT[:, i, :], bh32[:, 0, :], ALU.subtract)
            cp("act", biasHL[:, 6 + i, :], bh32[:, 1, :])
    for vv in (Vb, Vc):
        for g in range(2):
            a = vv[:, :, g * 128 + 64:g * 128 + 128]
            P.op("pool", lambda e, a=a: e.memset(a, 1.0), writes=[a])
    ckpt(2)

    def rmsnorm(ncols, gname, out3, out_f32=False):
        xT = xTv()
        b = bank("L8")
        for k in range(8):
            s = sq[:, k % 2, 0:ncols]
            xk = xT[:, k, 0:ncols]
            if k % 2 == 0:
                tt("pool", s, xk, xk, ALU.mult)
            else:
                act(s, xk, AF.Square)
            mm(ps[:, b, 0:ncols], onesm[:], s, k == 0, k == 7)
        act(rstd[:, 0:ncols], ps[:, b, 0:ncols], AF.Ln, bias=pc("eps"))
        act(rstd[:, 0:ncols], rstd[:, 0:ncols], AF.Exp, scale=-0.5)
        for k in range(8):
            stt(out3[:, k, 0:ncols], xT[:, k, 0:ncols], pc(gname, k), rstd[:, 0:ncols], ALU.mult, ALU.mult)

    sq8 = av(34 * K, 8 * K, BF16, 512)
    tmpS4 = av(16 * K, 8 * K, F32, 512)

    sq8b = av(26 * K, 8 * K, BF16, 512)

    def norm_sq(p, ncols, sqb=None):
        sqb = sq8 if sqb is None else sqb
        xT = xTv(p)
        for k in range(8):
            s_ = sqb[:, k, 0:ncols]
            xk = xT[:, k, 0:ncols]
            if k % 2 == 0:
                tt("pool", s_, xk, xk, ALU.mult)
            else:
                act(s_, xk, AF.Square)

    def norm_fin(p, ncols, gname, out3, sqb=None):
        sqb = sq8 if sqb is None else sqb
        xT = xTv(p)
        b = bank("L8")
        for k in range(8):
            mm(ps[:, b, 0:ncols], onesm[:], sqb[:, k, 0:ncols], k == 0, k == 7)
        act(rstd[:, 0:ncols], ps[:, b, 0:ncols], AF.Ln, bias=pc("eps"))
        act(rstd[:, 0:ncols], rstd[:, 0:ncols], AF.Exp, scale=-0.5)
        for k in range(8):
            stt(out3[:, k, 0:ncols], xT[:, k, 0:ncols], pc(gname, k), rstd[:, 0:ncols], ALU.mult, ALU.mult)

    def qknorm_rope(src, gcol, out, fill=None):
        def f(n):
            if fill is not None:
                for _ in range(n):
                    fill()
        cp("dve", rt[:, 0, :], src)
        act(sqc, rt[:, 0, :], AF.Square)
        f(1)
        b = bank("L8")
        mm(ps[:, b, :], blkb[:], sqc, True, True)
        act(rt[:, 1, :], ps[:, b, :], AF.Ln, bias=pc("eps"))
        act(rt[:, 1, :], rt[:, 1, :], AF.Exp, scale=-0.5)
        stt(knb, rt[:, 0, :], gcol, rt[:, 1, :], ALU.mult, ALU.mult)
        stt(rt[:, 2, :], rt[:, 0, :], gcol, rt[:, 1, :], ALU.mult, ALU.mult)
        f(1)
        b2 = bank("L8")
        mm(ps[:, b2, :], rotb[:], knb, True, True)
        tt("pool", rt[:, 3, :], rt[:, 2, :], ropeC[:], ALU.mult)
        tt("dve", rt[:, 0, :], ps[:, b2, :], ropeS[:], ALU.mult)
        tt("pool", out, rt[:, 3, :], rt[:, 0, :], ALU.add)

    def load_rope(t):
        dma("sp", ropeC[:], rope_d[0, :, t * T:(t + 1) * T], "ropeC", writes=[ropeC[:]])
        dma("sp", ropeS[:], rope_d[1, :, t * T:(t + 1) * T], "ropeS", writes=[ropeS[:]])

    def load_xT(src, name, lo, ncols, p=None):
        p = cur["p"] if p is None else p
        xT = xTv(p)
        dma("sp", xT[:, :, 0:ncols], src[:, :, lo:lo + ncols], ("xT", p), reads=xs_keys(name, lo, lo + ncols), writes=[xT[:, :, 0:ncols]])

    deferred = []

    def flush():
        while deferred:
            deferred.pop(0)()

    def xs_keys(name, lo, hi):
        return [(name, i) for i in range(lo // 2, (hi - 1) // 2 + 1)]

    for l in range(L):
        dma("sp", lnG[:], lngb_d[l, 0, :, :], "lnG", writes=[lnG[:]])
        dma("sp", lnB[:], lngb_d[l, 1, :, :], "lnB", writes=[lnB[:]])
        dma("sp", bS[:], bS_d[l, :, :, :], "bS", writes=[bS[:]])
        dma("pool", wsT[:], wsT_d[l, :, :, :], "wsT", writes=[wsT[:]])
        dma("sp", sinke[:], sinkb_d[l, :, :, :], "sinke", writes=[sinke[:]])
        act(sinke[:], sinke[:], AF.Exp)
        ckpt(20)

        ensure("win", l)
        wkv = wload(4096, winb, l, "win", 4).rearrange("p (a b) -> p a b", b=512)
        ckpt(21)
        if l > 0:
            load_xT(xsA, "xsA", 0, T)
        for t in range(NTILE):
            c0 = t * T
            xT = xTv()
            if l == 0:
                pump(int(os.environ.get("KPUMP", (CPL + NTILE - 1) // NTILE)))
            ckpt(30)
            if l == 0:
                for bq in range(4):
                    xi = xin[:, bq, :]
                    dma("sp", xi, x_d[c0 + bq * 128:c0 + (bq + 1) * 128, :], ("xin", bq), writes=[xi])
                    for hf in range(2):
                        b = bank("L8")
                        for kk in range(4):
                            k = hf * 4 + kk
                            tr(ps[:, b, kk * 128:(kk + 1) * 128], xi[:, k * 128:(k + 1) * 128])
                        cp("dve" if hf == 0 else "act", xT[:, hf * 4:hf * 4 + 4, bq * 128:(bq + 1) * 128],
                           ps[:, b, :].rearrange("p (a b) -> p a b", b=128))
                dma("sp", xsA[:, :, c0:c0 + T], xT[:, :, :], ("xTst", cur["p"]), reads=[xT[:, :, :]], writes=xs_keys("xsA", c0, c0 + T))
            ckpt(31)
            load_rope(t)
            rmsnorm(T, f"n1g{l}", xn)
            if l > 0 and t + 1 < NTILE:
                load_xT(xsA, "xsA", c0 + T, T, 1 - cur["p"])
            ckpt(32)
            fa = []

            def unit_kb(c0=c0):
                b = bank("L8")
                for k in range(8):
                    mm(ps[:, b, :], wkv[:, k, 0:128], xn[:, k, :], k == 0, k == 7)
                cp("act", KbT[:, c0:c0 + T], ps[:, b, :])

            def unit_v(bq, t=t):
                nb = t * 4 + bq
                b = bank("L8")
                for k in range(8):
                    mm(ps[:, b, 0:256], xn[:, k, bq * 128:(bq + 1) * 128], wkv[:, k, 256:512], k == 0, k == 7)
                cp("dve", Vb[:, nb, :].rearrange("p (g c) -> p g c", c=128)[:, :, 0:64],
                   ps[:, b, 0:128].rearrange("p (g c) -> p g c", c=64))
                cp("dve", Vc[:, nb, :].rearrange("p (g c) -> p g c", c=128)[:, :, 0:64],
                   ps[:, b, 128:256].rearrange("p (g c) -> p g c", c=64))

            fa.append(unit_kb)
            for bq in range(4):
                fa.append(lambda bq=bq: unit_v(bq))

            def fillA():
                if fa:
                    fa.pop(0)()

            b = bank("L8")
            for k in range(8):
                mm(ps[:, b, :], wkv[:, k, 128:256], xn[:, k, :], k == 0, k == 7)
            qknorm_rope(ps[:, b, :], pc(f"kg{l}"), KcT[:, c0:c0 + T], fillA)
            while fa:
                fillA()
            if l == 0 and t == NTILE // 2:
                setup_bias()
            cur["p"] ^= 1

        ckpt(4)
        ensure("wo", l)
        load_xT(xsA, "xsA", 0, T)
        for t in range(NTILE):
            c0 = t * T
            xT = xTv()
            load_rope(t)
            if t == 0:
                rmsnorm(T, f"n1g{l}", xn)
            pre_b = (lambda c0=c0, p=1 - cur["p"]: load_xT(xsA, "xsA", c0 + T, T, p)) if t + 1 < NTILE else None
            wcache = {}
            wcache[0] = wload(4096, winb, l, "win", 0).rearrange("p (a b) -> p a b", b=512)
            wq = wload(4096, winb, l, "win", 2).rearrange("p (a b) -> p a b", b=512)

            def wget(blk):
                if blk not in wcache:
                    wcache[blk] = wload(4096, winb, l, "win", blk).rearrange("p (a b) -> p a b", b=512)
                return wcache[blk]

            def unit_u(j):
                w = wget(0)
                b = bank("L8")
                for k in range(8):
                    mm(ps[:, b, :], w[:, k, j * 128:(j + 1) * 128], xn[:, k, :], k == 0, k == 7)
                act(uT[:, j, :], ps[:, b, :], AF.Gelu_apprx_tanh)

            vslots = [vg[:, 0, :], vg[:, 1, :], tmpS[:, 0, :], tmpS[:, 1, :]]

            def unit_v(bq):
                w = wget(3)
                b = bank("L8")
                for k in range(8):
                    mm(ps[:, b, :], xn[:, k, bq * 128:(bq + 1) * 128], w[:, k, :], k == 0, k == 7)
                act(vslots[bq], ps[:, b, :], AF.Gelu_apprx_tanh)

            def ln_all():
                for bq in range(4):
                    v_ = vslots[bq]
                    so = bq * 8
                    P.op("dve", lambda e, v_=v_, so=so: e.bn_stats(out=stat[:, so:so + 6], in_=v_), reads=[v_], writes=[stat[:, so:so + 6]])
                    P.op("dve", lambda e, so=so: e.bn_aggr(out=stat[:, so + 6:so + 8], in_=stat[:, so:so + 6]),
                         reads=[stat[:, so:so + 6]], writes=[stat[:, so + 6:so + 8]])
                for bq in range(4):
                    so = bq * 8
                    act(stat[:, so + 7:so + 8], stat[:, so + 7:so + 8], AF.Sqrt, bias=pc("eps"))
                for bq in range(4):
                    so = bq * 8
                    recip(stat[:, so + 7:so + 8], stat[:, so + 7:so + 8])
                for bq in range(4):
                    v_ = vslots[bq]
                    so = bq * 8
                    ts("dve", v_, v_, stat[:, so + 6:so + 7], stat[:, so + 7:so + 8], ALU.subtract, ALU.mult)
                    tt("pool", v_, v_, lnG[:], ALU.mult)
                    tt("pool", vln[:, bq, :], v_, lnB[:], ALU.add)

            def unit_qb(j):
                w = wget(1)
                b = bank("L8")
                for k in range(8):
                    mm(ps[:, b, :], w[:, k, j * 128:(j + 1) * 128], xn[:, k, :], k == 0, k == 7)
                act(qbT[:, j, :], ps[:, b, :], AF.Identity, scale=0.125)

            for j in range(4):
                unit_u(j)
            for j in range(4):
                unit_v(j)
            ln_all()
            fb = []
            for j in range(4):
                fb.append(None)
                fb.append(lambda j=j: unit_qb(j))

            def fillB():
                if fb:
                    u_ = fb.pop(0)
                    if u_ is not None:
                        u_()

            for j in range(4):
                b = bank("L8")
                for k in range(8):
                    mm(ps[:, b, :], wq[:, k, j * 128:(j + 1) * 128], xn[:, k, :], k == 0, k == 7)
                qknorm_rope(ps[:, b, :], pc(f"qg{l}"), qcT[:, j, :], fillB)
            while fb:
                fillB()
            flush()
            if pre_b is not None:
                pre_b()
            if l + 1 < L:
                pump((CPL + NTILE - 1) // NTILE)
            ckpt(44)
            for j in range(4):
                b = bank("L8")
                for bq in range(4):
                    for hh in range(2):
                        g = 2 * j + hh
                        mm(ps[hh * 64:(hh + 1) * 64, b, bq * 128:(bq + 1) * 128], vln[:, bq, g * 64:(g + 1) * 64], wsT[:, g, :], True, True)
                tm = tmpS[:, j % 2, :]
                tt("dve", tm.rearrange("p (a b) -> p a b", b=128), ps[:, b, :].rearrange("p (a b) -> p a b", b=128),
                   bS[:, j:j + 1, :].to_broadcast([128, 4, 128]), ALU.add)
                tt("pool", oT[:, j, :], tm, uT[:, j, :], ALU.mult)
            ckpt(45)
            for qi in range(4):
                n = t * 4 + qi
                qs = slice(qi * 128, (qi + 1) * 128)
                chunks = [c for c in (0, 1, 2) if 0 <= n + c - 1 < NB]
                bO = [4, 5] if qi % 2 == 0 else [6, 7]
                nch = len(chunks)

                def SB(ci):
                    kb = n + chunks[ci] - 1
                    c = chunks[ci]
                    for g in range(2):
                        pr = slice(g * 64, (g + 1) * 64)
                        bk = ps[:, (ci % 2) * 2 + g, :]
                        mm(bk, KbT[pr, kb * 128:(kb + 1) * 128], qbT[pr, :, qs], True, False)
                        mm(bk, identb[:], biasHL[:, c * 2 + g, :], False, False)
                        mm(bk, identb[:], biasHL[:, 6 + c * 2 + g, :], False, True)

                def DB(ci):
                    pass

                def EB(ci):
                    c = chunks[ci]
                    edge = (n == HB - 1 and c == 2) or (n == HB and c == 0)
                    s2 = (ci % 2) * 2
                    act(PT[:, s2:s2 + 2, :].rearrange("p a b -> p (a b)"), ps[:, s2:s2 + 2, :].rearrange("p a b -> p (a b)"),
                        AF.Exp, bias=pc("cmask") if edge else pc("zero"))

                def PVB(ci):
                    kb = n + chunks[ci] - 1
                    for g in range(2):
                        mm(ps[:, bO[g], :], Vb[:, kb, g * 128:(g + 1) * 128], PT[:, (ci % 2) * 2 + g, :], ci == 0, ci == nch - 1)

                SB(0)
                if nch > 1:
                    SB(1)
                DB(0)
                for ci in range(nch):
                    EB(ci)
                    PVB(ci)
                    if ci + 2 < nch:
                        SB(ci + 2)
                    if ci + 1 < nch:
                        DB(ci + 1)
                for g in range(2):
                    r_ = rc[64:128, g, :]
                    tt("dve", r_, ps[64:128, bO[g], :], sinke[64:128, g, :], ALU.add)
                    act(r_, r_, AF.Ln)
                    act(r_, r_, AF.Exp, scale=-1.0)
                    tt("dve", oT[g * 64:(g + 1) * 64, 4:8, qs], ps[0:64, bO[g], :].rearrange("p (a b) -> p a b", b=128),
                       r_.rearrange("p (a b) -> p a b", b=128), ALU.mult)
            ckpt(46)
            for qi in range(4):
                n = t * 4 + qi
                qs = slice(qi * 128, (qi + 1) * 128)
                bO = [4, 5] if qi % 2 == 0 else [6, 7]

                def SC(kb):
                    for g in range(2):
                        pr = slice(g * 64, (g + 1) * 64)
                        mm(ps[:, (kb % 2) * 2 + g, :], KcT[pr, kb * 128:(kb + 1) * 128], qcT[pr, :, qs], True, True)

                def EC(kb):
                    cross = (n < HB) != (kb < HB)
                    s2 = (kb % 2) * 2
                    act(PT[:, s2:s2 + 2, :].rearrange("p a b -> p (a b)"), ps[:, s2:s2 + 2, :].rearrange("p a b -> p (a b)"),
                        AF.Exp, bias=pc("cmask") if cross else pc("zero"), scale=0.125)

                def PVC(kb):
                    for g in range(2):
                        mm(ps[:, bO[g], :], Vc[:, kb, g * 128:(g + 1) * 128], PT[:, (kb % 2) * 2 + g, :], kb == 0, kb == NB - 1)

                SC(0)
                SC(1)
                for kb in range(NB):
                    EC(kb)
                    PVC(kb)
                    if kb + 2 < NB:
                        SC(kb + 2)
                for g in range(2):
                    r_ = rc[64:128, g, :]
                    recip(r_, ps[64:128, bO[g], :])
                    tt("dve", oT[g * 64:(g + 1) * 64, 8:12, qs], ps[0:64, bO[g], :].rearrange("p (a b) -> p a b", b=128),
                       r_.rearrange("p (a b) -> p a b", b=128), ALU.mult)
            if pre_b is not None:
                norm_sq(1 - cur["p"], T, sq8b)
            ckpt(47)
            for kq in range(2):
                for nbr in range(3):
                    wg_ = wload(4096, wgb, l, "wg", nbr * 2 + kq).rearrange("p (a b) -> p a b", b=512)
                    wb_ = wload(2048, wbrb, l, "wbr", nbr * 2 + kq).rearrange("p (a b) -> p a b", b=512)
                    for kk in range(4):
                        k = kq * 4 + kk
                        cs = slice(kk * 128, (kk + 1) * 128)
                        bY = bank("L8")
                        for j in range(4):
                            mm(ps[:, bY, :], wb_[:, j, cs], oT[:, nbr * 4 + j, :], j == 0, j == 3)
                        bG = bank("L8")
                        for i in range(8):
                            mm(ps[:, bG, :], wg_[:, i, cs], xn[:, i, :], i == 0, i == 7)
                        g_ = gt[:, kk % 2, :]
                        act(g_, ps[:, bG, :], AF.Sigmoid, bias=pc(f"bg{l}", nbr * 8 + k))
                        if nbr == 0:
                            tt("dve", acc[:, kk, :], g_, ps[:, bY, :], ALU.mult)
                        elif nbr == 1:
                            tt("dve", g_, g_, ps[:, bY, :], ALU.mult)
                            tt("pool", acc[:, kk, :], acc[:, kk, :], g_, ALU.add)
                        else:
                            tt("dve", g_, g_, ps[:, bY, :], ALU.mult)
                            tt("pool", mg[:, k, :], acc[:, kk, :], g_, ALU.add)
            if pre_b is not None:
                norm_fin(1 - cur["p"], T, f"n1g{l}", xn, sq8b)
            ckpt(48)
            for kq in range(2):
                w = wload(4096, wob, l, "wo", kq).rearrange("p (a b) -> p a b", b=512)
                for kk in range(4):
                    k = kq * 4 + kk
                    b = bank("L8")
                    for i in range(8):
                        mm(ps[:, b, :], w[:, i, kk * 128:(kk + 1) * 128], mg[:, i, :], i == 0, i == 7)
                    tt("dve", xT[:, k, :], ps[:, b, :], xT[:, k, :], ALU.add)
            deferred.append(lambda c0=c0, xT=xT, p=cur["p"]: dma("sp", xsB[:, :, c0:c0 + T], xT[:, :, :], ("xTst", p), reads=[xT[:, :, :]],
                                                                 writes=xs_keys("xsB", c0, c0 + T)))
            cur["p"] ^= 1
        flush()

        ckpt(5)
        ensure("wdn", l)
        tiles = []
        for hbase in (0, H):
            a = hbase
            while a < hbase + H:
                bnd = min(a + FT, hbase + H)
                tiles.append((a, bnd))
                a = bnd
        def cgeom(a, bnd):
            lo = a - 1 if a > 0 else a
            hi = bnd + 1 if bnd < NT else bnd
            return lo, hi

        lo_, hi_ = cgeom(*tiles[0])
        load_xT(xsB, "xsB", lo_, hi_ - lo_)
        for ti, (a, bnd) in enumerate(tiles):
            lo, hi = cgeom(a, bnd)
            ncols = hi - lo
            oi = a - lo
            wo_ = bnd - a
            x0 = 0 if a > 0 else 1
            y0 = 0 if bnd < NT else 1
            xT = xTv()
            if ti == 0:
                rmsnorm(ncols, f"n2g{l}", xn)
            pre_c = None
            ncn = 0
            if ti + 1 < len(tiles):
                lo_, hi_ = cgeom(*tiles[ti + 1])
                ncn = hi_ - lo_
                pre_c = lambda lo_=lo_, hi_=hi_, p=1 - cur["p"]: load_xT(xsB, "xsB", lo_, hi_ - lo_, p)
            for c in range(NFC):
                wu = wload(2048, wupb, l, "wup", c).rearrange("p (a b) -> p a b", b=256)
                if c == 2:
                    flush()
                if c == 8 and pre_c is not None:
                    pre_c()
                if c == 14 and pre_c is not None:
                    norm_sq(1 - cur["p"], ncn)
                hb_ = []
                for gv in range(2):
                    b = bank("L8")
                    for i in range(8):
                        mm(ps[:, b, 0:ncols], wu[:, i, gv * 128:(gv + 1) * 128], xn[:, i, 0:ncols], i == 0, i == 7)
                    if a == H and a > 0:
                        ts("dve", ps[:, b, 0:1], ps[:, b, 0:1], pc("m01"), None, ALU.mult)
                    if bnd == H:
                        ts("dve", ps[:, b, ncols - 1:ncols], ps[:, b, ncols - 1:ncols], pc("m01"), None, ALU.mult)
                    ch = gv * NFC + c
                    h_ = hc[:, (c % 2) * 2 + gv, :]
                    act(h_[:, 0:wo_], ps[:, b, oi:oi + wo_], AF.Identity, bias=pc(f"cb{l}", ch), scale=pc(f"cw1{l}", ch))
                    if wo_ - x0 > 0:
                        stt(h_[:, x0:wo_], ps[:, b, oi + x0 - 1:oi + wo_ - 1], pc(f"cw0{l}", ch), h_[:, x0:wo_], ALU.mult, ALU.add)
                    if wo_ - y0 > 0:
                        stt(h_[:, 0:wo_ - y0], ps[:, b, oi + 1:oi + 1 + wo_ - y0], pc(f"cw2{l}", ch), h_[:, 0:wo_ - y0], ALU.mult, ALU.add)
                    hb_.append(h_)
                s_ = sg[:, c % 2, :]
                act(s_[:, 0:wo_], hb_[0][:, 0:wo_], AF.Silu)
                tt("pool", actb[:, c, 0:wo_], s_[:, 0:wo_], hb_[1][:, 0:wo_], ALU.mult)
            if pre_c is not None:
                norm_fin(1 - cur["p"], ncn, f"n2g{l}", xn)
            for k in range(8):
                wd = wload(NFC * 128, wdnb, l, "wdn", k).rearrange("p (a b) -> p a b", b=128)
                b = bank("L8")
                for c in range(NFC):
                    mm(ps[:, b, 0:wo_], wd[:, c, :], actb[:, c, 0:wo_], c == 0, c == NFC - 1)
                tt("dve", xT[:, k, oi:oi + wo_], ps[:, b, 0:wo_], xT[:, k, oi:oi + wo_], ALU.add)
            deferred.append(lambda a=a, bnd=bnd, xT=xT, oi=oi, wo_=wo_, p=cur["p"]: dma(
                "sp", xsA[:, :, a:bnd], xT[:, :, oi:oi + wo_], ("xTst", p), reads=[xT[:, :, oi:oi + wo_]], writes=xs_keys("xsA", a, bnd)))
            cur["p"] ^= 1
        flush()

    ckpt(6)
    load_xT(xsA, "xsA", 0, T)
    for t in range(NTILE):
        c0 = t * T
        rmsnorm(T, "fing", yT)
        if t + 1 < NTILE:
            load_xT(xsA, "xsA", c0 + T, T, 1 - cur["p"])
        cur["p"] ^= 1
        for bq in range(4):
            yo_ = yo[:, bq % 2, :]
            for hf in range(2):
                b = bank("L8")
                for kk in range(4):
                    k = hf * 4 + kk
                    tr(ps[:, b, kk * 128:(kk + 1) * 128], yT[:, k, bq * 128:(bq + 1) * 128])
                cp("dve" if hf == 0 else "act", yo_[:, hf * 512:(hf + 1) * 512], ps[:, b, :])
            dma("sp", y_d[c0 + bq * 128:c0 + (bq + 1) * 128, :], yo_, ("yst", bq % 2), reads=[yo_], writes=[("y", t, bq)])

    counts = P.finalize()
    CH = P.CH
    sem_stack = ExitStack()
    esem = {}
    for e in P.ENGS:
        n = (counts[e] + CH - 1) // CH
        esem[e] = [sem_stack.enter_context(nc.semaphore(f"s_{e}_{i}")) for i in range(max(n, 1))]
    dsem = {}
    for k in P.dma_cum:
        dsem[k] = sem_stack.enter_context(nc.semaphore(f"d{len(dsem)}"))

    def emit_engine(ename, eng):
        waited_c = {e: 0 for e in P.ENGS}
        waited_d = {}
        for o in P.ops[ename]:
            need = {}
            for d in o.deps:
                if d.count > need.get(d.eng, 0):
                    need[d.eng] = d.count
            for se, cnt in need.items():
                if cnt > waited_c[se]:
                    eng.wait_ge(esem[se][(cnt - 1) // CH], (cnt - 1) % CH + 1)
                    waited_c[se] = cnt
            for sk, val in o.dma_waits.items():
                if val > waited_d.get(sk, 0):
                    eng.wait_ge(dsem[sk], val)
                    waited_d[sk] = val
            if o.fn is None:
                continue
            ins = o.fn(eng)
            if o.is_dma:
                ins.then_inc(dsem[o.semkey], 16)
            elif o.signal:
                ins.then_inc(esem[ename][(o.count - 1) // CH], 1)
        if ename == "sp":
            for sk, val in P.dma_cum.items():
                eng.wait_ge(dsem[sk], val)

    with nc.Block() as block:
        @block.tensor
        def _(e):
            emit_engine("pe", e)

        @block.scalar
        def _(e):
            emit_engine("act", e)

        @block.vector
        def _(e):
            emit_engine("dve", e)

        @block.gpsimd
        def _(e):
            emit_engine("pool", e)

        @block.sync
        def _(e):
            emit_engine("sp", e)

    sem_stack.close()
    es.close()
    return nc


_PROG_CACHE = {}


def run_cores(core_tokens, two_seq_flags, params, NT, L):
    key = (NT, L)
    if key not in _PROG_CACHE:
        _PROG_CACHE[key] = build_program(NT, L)
    nc = _PROG_CACHE[key]
    shared = _shared_inputs(params, L)
    rope2 = _rope_host(NT, True)
    rope1 = _rope_host(NT, False)
    in_maps = []
    for xt, two in zip(core_tokens, two_seq_flags):
        m = dict(shared)
        m["x"] = np.ascontiguousarray(xt, dtype=np.float32)
        m["pcol"] = _pcol_host(params, L, two)
        m["rope"] = rope2 if two else rope1
        in_maps.append(m)
    res = run_bass_kernel_spmd(nc, in_maps, core_ids=list(range(len(in_maps))))
    return [np.asarray(r["y"]) for r in res.results]


def kernel(x_prompt, x_sample, rel_bias, norm1_g, w_in, ln_v_g, ln_v_b, w_spatial, b_spatial, sink,
           q_norm_g, k_norm_g, w_gate, b_gate, w_branch, w_out, norm2_g, w_up, conv_w, conv_b, w_down, final_g):
    params = dict(rel_bias=rel_bias, norm1_g=norm1_g, w_in=w_in, ln_v_g=ln_v_g, ln_v_b=ln_v_b, w_spatial=w_spatial,
                  b_spatial=b_spatial, sink=sink, q_norm_g=q_norm_g, k_norm_g=k_norm_g, w_gate=w_gate, b_gate=b_gate,
                  w_branch=w_branch, w_out=w_out, norm2_g=norm2_g, w_up=w_up, conv_w=conv_w, conv_b=conv_b,
                  w_down=w_down, final_g=final_g)
    params = {k: np.asarray(v, dtype=np.float32) for k, v in params.items()}
    xp = np.asarray(x_prompt, dtype=np.float32)
    xs = np.asarray(x_sample, dtype=np.float32)
    NT = 4096
    toks = [xp[2 * c:2 * c + 2].reshape(NT, D) for c in range(4)] + [xs[c] for c in range(4)]
    flags = [True] * 4 + [False] * 4
    ys = run_cores(toks, flags, params, NT, 2)
    y_prompt = np.stack(ys[0:4]).reshape(8, 2048, D).astype(np.float32)
    y_sample = np.stack(ys[4:8]).reshape(4, 4096, D).astype(np.float32)
    return (y_prompt, y_sample)
```

```python
import math
from contextlib import ExitStack

import numpy as np
import concourse.bass as bass
import concourse.mybir as mybir
from concourse.bass_utils import run_bass_kernel_spmd

F32 = mybir.dt.float32
BF16 = mybir.dt.bfloat16
AF = mybir.ActivationFunctionType
ALU = mybir.AluOpType

D = 1024
KD = 8
HD = 64
DIN = 2560
DG = 3072
DFF = 2816
NFC = 22
EPS = 1e-6
NEG = -30000.0
T = 512
FT = 410
NS = 3


class _Op:
    __slots__ = ("eng", "fn", "idx", "deps", "dma_waits", "is_dma", "semkey", "cum", "signal", "count")


class Prog:
    ENGS = ("pe", "act", "dve", "pool", "sp")
    GRAN = 512
    CH = 6000

    def __init__(self):
        self.ops = {e: [] for e in self.ENGS}
        self.res = {}
        self.dma_cum = {}

    def _keys(self, a):
        if not isinstance(a, bass.AP):
            return [a]
        sp = str(a.space)
        name = a.tensor.name
        es = mybir.dt.size(a.dtype)
        dims = a.ap
        pstep = dims[0][0]
        if sp == "PSUM":
            gran = 2048
        else:
            gran = self.GRAN
        foff = (a.offset % pstep) if pstep > 0 else a.offset
        starts = [foff]
        for (st, cnt) in dims[1:-1]:
            starts = [s + i * st for s in starts for i in range(cnt)]
        lst, lcnt = dims[-1] if len(dims) > 1 else (1, 1)
        ln = (abs(lst) * (lcnt - 1) + 1)
        ks = set()
        for s in starts:
            b0 = (s * es) // gran
            b1 = ((s + ln) * es - 1) // gran
            for b in range(b0, b1 + 1):
                ks.add((name, b))
        return list(ks)

    def op(self, eng, fn, reads=(), writes=(), dma=None):
        if getattr(self, "frozen", False):
            return None
        o = _Op()
        o.eng = eng
        o.fn = fn
        o.idx = len(self.ops[eng])
        o.deps = set()
        o.dma_waits = {}
        o.is_dma = dma is not None
        o.semkey = dma
        o.signal = False
        o.count = 0
        rk = [k for r in reads for k in self._keys(r)]
        wk = [k for w in writes for k in self._keys(w)]
        for k in rk:
            st = self.res.get(k)
            if st is not None and st[0] is not None:
                o.deps.add(st[0])
        for k in wk:
            st = self.res.get(k)
            if st is not None:
                if st[0] is not None:
                    o.deps.add(st[0])
                for r in st[1]:
                    o.deps.add(r)
        o.deps.discard(o)
        for d in o.deps:
            if d.is_dma:
                o.dma_waits[d.semkey] = self.dma_cum[d.semkey]
        if o.is_dma:
            self.dma_cum[dma] = self.dma_cum.get(dma, 0) + 16
            o.cum = self.dma_cum[dma]
        for k in rk:
            st = self.res.setdefault(k, [None, []])
            if not o.is_dma:
                st[1] = [r for r in st[1] if r.is_dma or r.eng != eng]
            st[1].append(o)
        for k in wk:
            self.res[k] = [o, []]
        self.ops[eng].append(o)
        return o

    def finalize(self):
        for e in self.ENGS:
            for o in self.ops[e]:
                keep = []
                for d in o.deps:
                    if d.is_dma:
                        continue
                    if d.eng == o.eng and not o.is_dma:
                        if o.eng == "pe":
                            continue
                        if o.idx - d.idx > 3:
                            continue
                    d.signal = True
                    keep.append(d)
                o.deps = keep
        for e in self.ENGS:
            c = 0
            for o in self.ops[e]:
                if o.signal:
                    c += 1
                    o.count = c
        return {e: max([o.count for o in self.ops[e]] + [0]) for e in self.ENGS}


def _pkn(w, kc):
    n = w.shape[1]
    return np.ascontiguousarray(w.reshape(kc, 128, n).transpose(1, 0, 2))


def _t5_bucket_np(rel):
    half = 16
    max_exact = 8
    ret = np.where(rel > 0, half, 0)
    n = np.abs(rel)
    nf = np.maximum(n, 1).astype(np.float32)
    large = max_exact + (np.log(nf / np.float32(max_exact)) / np.float32(math.log(128 / max_exact))
                         * np.float32(half - max_exact)).astype(np.int32)
    large = np.minimum(large, half - 1)
    return ret + np.where(n < max_exact, n, large)


_QPERM = np.concatenate([np.concatenate([j * 64 + np.arange(64), (4 + j) * 64 + np.arange(64)]) for j in range(4)])


def _pcol_layout(L):
    off = {}
    c = 0

    def add(name, n):
        nonlocal c
        off[name] = c
        c += n

    add("zero", 1)
    add("eps", 1)
    add("cmask", 1)
    add("m01", 1)
    for l in range(L):
        add(f"n1g{l}", 8)
        add(f"n2g{l}", 8)
        add(f"bg{l}", 24)
        add(f"cw0{l}", 44)
        add(f"cw1{l}", 44)
        add(f"cw2{l}", 44)
        add(f"cb{l}", 44)
        add(f"qg{l}", 1)
        add(f"kg{l}", 1)
    add("fing", 8)
    return off, c


def _shared_inputs(p, L):
    f = np.float32
    out = {}
    wl = {k: [] for k in ("w_in", "w_gate", "w_br", "w_out", "w_up", "w_down")}
    for l in range(L):
        wi = p["w_in"][l]
        cat = np.concatenate([wi[:, 0:512], wi[:, 1024:1536][:, _QPERM], wi[:, 1792:2304][:, _QPERM],
                              wi[:, 512:1024], wi[:, 1536:1664], wi[:, 2304:2432], wi[:, 1664:1792], wi[:, 2432:2560]], axis=1)
        wl["w_in"].append(_pkn(cat, 8).reshape(128, 8, 5, 512).transpose(0, 2, 1, 3).reshape(128, 5, 4096))
        wl["w_gate"].append(_pkn(p["w_gate"][l], 8).reshape(128, 8, 6, 512).transpose(0, 2, 1, 3).reshape(128, 6, 4096))
        rows = np.concatenate([p["w_branch"][l][0], p["w_branch"][l][1][_QPERM], p["w_branch"][l][2][_QPERM]], axis=0)
        wl["w_br"].append(_pkn(rows, 12).reshape(128, 3, 4, 2, 512).transpose(0, 1, 3, 2, 4).reshape(128, 6, 2048))
        wl["w_out"].append(_pkn(p["w_out"][l], 8).reshape(128, 8, 2, 512).transpose(0, 2, 1, 3).reshape(128, 2, 4096))
        up = _pkn(p["w_up"][l], 8)
        gv = np.stack([up[:, :, 0:DFF].reshape(128, 8, NFC, 128), up[:, :, DFF:].reshape(128, 8, NFC, 128)], axis=3)
        wl["w_up"].append(gv.transpose(0, 2, 1, 3, 4).reshape(128, NFC, 2048))
        wl["w_down"].append(_pkn(p["w_down"][l], NFC).reshape(128, NFC, 8, 128).transpose(0, 2, 1, 3).reshape(128, 8, NFC * 128))
    for k in wl:
        out[k] = np.ascontiguousarray(np.stack(wl[k])).astype(f)
    out["lngb"] = np.stack([np.stack([np.broadcast_to(p["ln_v_g"][l], (128, 512)),
                                      np.broadcast_to(p["ln_v_b"][l], (128, 512))]) for l in range(L)]).astype(f)
    bs = np.zeros((L, 128, 4, 128), f)
    for l in range(L):
        for j in range(4):
            bs[l, 0:64, j, :] = p["b_spatial"][l][2 * j][None, :]
            bs[l, 64:128, j, :] = p["b_spatial"][l][2 * j + 1][None, :]
    out["bS"] = bs
    out["wsT"] = np.ascontiguousarray(np.stack([p["w_spatial"][l].transpose(2, 0, 1) for l in range(L)])).astype(f)
    sk = np.zeros((L, 128, 2, 512), f)
    for l in range(L):
        for g in range(2):
            for hl in range(4):
                sk[l, :, g, hl * 128:(hl + 1) * 128] = p["sink"][l][g * 4 + hl]
    out["sinkb"] = sk
    out["ident"] = np.eye(128, dtype=f)
    out["onesm"] = np.full((128, 128), 1.0 / 1024.0, f)
    blk = np.zeros((128, 128), f)
    blk[0:64, 0:64] = 1.0 / 64.0
    blk[64:128, 64:128] = 1.0 / 64.0
    out["blk"] = blk
    R = np.zeros((128, 128), f)
    for base in (0, 32, 64, 96):
        for e in range(16):
            R[base + e + 16, base + e] = -1.0
            R[base + e, base + e + 16] = 1.0
    out["rotm"] = R
    oh = np.zeros((33, 3, 256), f)
    for c in range(3):
        s = np.arange(255)
        rel = 128 * (c - 1) + 127 - s
        b = _t5_bucket_np(rel.astype(np.int32))
        oh[b, c, s] = 1.0
        oh[32, c, s] = np.where(np.abs(rel) <= 128, 0.0, NEG)
    out["ohm"] = oh
    rr = np.ones((33, 8, 128), f)
    rr[0:32] = p["rel_bias"][:, :, None]
    out["relrep"] = rr
    return out


def _pcol_host(p, L, two_seq):
    off, n = _pcol_layout(L)
    pc = np.zeros((128, n), np.float32)
    pc[:, off["eps"]] = EPS
    pc[:, off["cmask"]] = NEG if two_seq else 0.0
    pc[:, off["m01"]] = 0.0 if two_seq else 1.0

    def col8(v):
        return v.reshape(-1, 128).T

    for l in range(L):
        pc[:, off[f"n1g{l}"]:off[f"n1g{l}"] + 8] = col8(p["norm1_g"][l])
        pc[:, off[f"n2g{l}"]:off[f"n2g{l}"] + 8] = col8(p["norm2_g"][l])
        pc[:, off[f"bg{l}"]:off[f"bg{l}"] + 24] = col8(p["b_gate"][l])
        for j in range(3):
            pc[:, off[f"cw{j}{l}"]:off[f"cw{j}{l}"] + 44] = col8(p["conv_w"][l][j])
        pc[:, off[f"cb{l}"]:off[f"cb{l}"] + 44] = col8(p["conv_b"][l])
        pc[:, off[f"qg{l}"]] = np.tile(p["q_norm_g"][l], 2)
        pc[:, off[f"kg{l}"]] = np.tile(p["k_norm_g"][l], 2)
    pc[:, off["fing"]:off["fing"] + 8] = col8(p["final_g"])
    return pc


def _rope_host(NT, two_seq):
    pos = np.arange(NT)
    if two_seq:
        pos = pos % (NT // 2)
    row = (pos // 64).astype(np.float32)
    col = (pos % 64).astype(np.float32)
    inv = (np.float32(10000.0) ** (-np.arange(16, dtype=np.float32) / np.float32(16))).astype(np.float32)
    tab = np.zeros((2, 128, NT), np.float32)
    for pp in range(128):
        d = pp % 64
        ax = row if d < 32 else col
        i = (d % 32) % 16
        ang = (ax * inv[i]).astype(np.float32)
        tab[0, pp] = np.cos(ang)
        tab[1, pp] = np.sin(ang)
    return tab


def build_program(NT, L):
    NB = NT // 128
    NTILE = NT // T
    H = NT // 2
    HB = NB // 2
    off, NPC = _pcol_layout(L)

    nc = bass.Bass("TRN2", target_bir_lowering=False)
    P = Prog()
    import os
    STOP = int(os.environ.get("KSTOP", "99"))

    def ckpt(n):
        if STOP == n:
            P.frozen = True

    def din(name, shape, dt=F32):
        return nc.dram_tensor(name, list(shape), dt, kind="ExternalInput")

    x_d = din("x", [NT, D])
    win_d = din("w_in", [L, 128, 5, 4096])
    wg_d = din("w_gate", [L, 128, 6, 4096])
    wbr_d = din("w_br", [L, 128, 6, 2048])
    wo_d = din("w_out", [L, 128, 2, 4096])
    wup_d = din("w_up", [L, 128, NFC, 2048])
    wdn_d = din("w_down", [L, 128, 8, NFC * 128])
    lngb_d = din("lngb", [L, 2, 128, 512])
    bS_d = din("bS", [L, 128, 4, 128])
    wsT_d = din("wsT", [L, 128, 8, 128])
    sinkb_d = din("sinkb", [L, 128, 2, 512])
    ident_d = din("ident", [128, 128])
    onesm_d = din("onesm", [128, 128])
    blk_d = din("blk", [128, 128])
    rotm_d = din("rotm", [128, 128])
    ohm_d = din("ohm", [33, 3, 256])
    relrep_d = din("relrep", [33, 8, 128])
    pcol_d = din("pcol", [128, NPC])
    rope_d = din("rope", [2, 128, NT])
    y_d = nc.dram_tensor("y", [NT, D], F32, kind="ExternalOutput")

    def dint(name, shape, dt):
        return nc.dram_tensor(name, list(shape), dt, kind="Internal")

    winb = dint("winb", [L, 128, 5, 4096], BF16)
    wgb = dint("wgb", [L, 128, 6, 4096], BF16)
    wbrb = dint("wbrb", [L, 128, 6, 2048], BF16)
    wob = dint("wob", [L, 128, 2, 4096], BF16)
    wupb = dint("wupb", [L, 128, NFC, 2048], BF16)
    wdnb = dint("wdnb", [L, 128, 8, NFC * 128], BF16)
    xsA = dint("xsA", [128, 8, NT], F32)
    xsB = dint("xsB", [128, 8, NT], F32)
    tabx = dint("tabx", [3, 128, 8, 256], F32)

    es = ExitStack()

    def sb(name, shape, dt):
        return es.enter_context(nc.sbuf_tensor("sb_" + name, list(shape), dt))

    KbT = sb("KbT", [128, NT], BF16)
    KcT = sb("KcT", [128, NT], BF16)
    Vb = sb("Vb", [128, NB, 256], BF16)
    Vc = sb("Vc", [128, NB, 256], BF16)
    biasHL = sb("biasHL", [128, 12, 512], BF16)
    identb = sb("identb", [128, 128], BF16)
    pcol = sb("pcol", [128, NPC], F32)
    ident = sb("ident", [128, 128], F32)
    onesm = sb("onesm", [128, 128], BF16)
    blkb = sb("blkb", [128, 128], BF16)
    rotb = sb("rotb", [128, 128], BF16)
    lnG = sb("lnG", [128, 512], F32)
    lnB = sb("lnB", [128, 512], F32)
    bS = sb("bS", [128, 4, 128], F32)
    wsT = sb("wsT", [128, 8, 128], BF16)
    sinke = sb("sinke", [128, 2, 512], F32)
    ropeC = sb("ropeC", [128, 512], F32)
    ropeS = sb("ropeS", [128, 512], F32)
    xTd = sb("xTd", [128, 2, 4096], F32)
    cur = {"p": 0}

    def xTv(p=None):
        p = cur["p"] if p is None else p
        return xTd[:, p, :].rearrange("q (a b) -> q a b", b=512)
    xn = sb("xn", [128, 8, 512], BF16)
    sq = sb("sq", [128, 2, 512], BF16)
    rstd = sb("rstd", [128, 512], F32)
    wbuf = sb("wbuf", [128, NS * 4096], BF16)
    stat = sb("stat", [128, 32], F32)
    ARENA = 60 * 1024 // 2
    arena = sb("arena", [128, ARENA], BF16)
    ps = es.enter_context(nc.psum_tensor("ps", [128, 8, 512], F32))

    def av(byte_off, nbytes, dt, shape3=None):
        a = arena[:, byte_off // 2:(byte_off + nbytes) // 2]
        if dt == F32:
            a = a.bitcast(F32)
        if shape3 is not None:
            a = a.rearrange("p (a b) -> p a b", b=shape3)
        return a

    K = 1024
    uT = av(0, 8 * K, F32, 512)
    vg = av(8 * K, 4 * K, F32, 512)
    vln = av(12 * K, 4 * K, BF16, 512)
    rt = av(16 * K, 8 * K, F32, 512)
    knb = av(24 * K, 1 * K, BF16)
    sqc = av(25 * K, 1 * K, BF16)
    mg = av(0, 8 * K, BF16, 512)
    gt = av(8 * K, 4 * K, F32, 512)
    acc = av(12 * K, 8 * K, F32, 512)
    qbT = av(26 * K, 4 * K, BF16, 512)
    qcT = av(30 * K, 4 * K, BF16, 512)
    oT = av(34 * K, 12 * K, BF16, 512)
    tmpS = av(46 * K, 4 * K, F32, 512)
    PT = av(50 * K, 6 * K, BF16, 512)
    rc = av(56 * K, 4 * K, F32, 512)
    actb = av(0, 22 * K, BF16, 512)
    hc = av(22 * K, 8 * K, F32, 512)
    sg = av(30 * K, 4 * K, F32, 512)
    yT = av(0, 16 * K, F32, 512)
    yo = av(16 * K, 8 * K, F32, 1024)
    xin = av(26 * K, 16 * K, F32, 1024)
    trep = av(0, 8 * K, F32, 256)
    biasT = av(26 * K, 12 * K, F32, 512)
    bh32 = av(38 * K, 4 * K, F32, 512)
    relrep_s = av(8 * K, 4 * K, F32, 128)
    ohm_s = av(12 * K, 3 * K, F32, 256)

    def pc(name, i=0):
        c = off[name] + i
        return pcol[:, c:c + 1]

    rot_state = {"L8": 0, "S": 0, "O": 0, "L": 0}

    def bank(role):
        sets = {"L8": [0, 1, 2, 3, 4, 5, 6, 7], "S": [0, 1, 2, 3], "O": [4, 5], "L": [6, 7]}[role]
        b = sets[rot_state[role] % len(sets)]
        rot_state[role] += 1
        return b

    def dma(q, out, in_, semkey, reads=(), writes=()):
        return P.op(q, lambda e, out=out, in_=in_: e.dma_start(out=out, in_=in_), reads=reads, writes=writes, dma=semkey)

    def mm(out, lhsT, rhs, start, stop):
        return P.op("pe", lambda e: e.matmul(out, lhsT, rhs, start=start, stop=stop), reads=[lhsT, rhs], writes=[out])

    def tr(out, in_):
        return P.op("pe", lambda e: e.transpose(out, in_, ident[:]), reads=[in_, ident[:]], writes=[out])

    def act(out, in_, func, bias=None, scale=1.0, extra_reads=()):
        rd = [in_] + list(extra_reads)
        if isinstance(bias, bass.AP):
            rd.append(bias)
        if isinstance(scale, bass.AP):
            rd.append(scale)
        kw = {}
        if bias is not None:
            kw["bias"] = bias
        return P.op("act", lambda e: e.activation(out=out, in_=in_, func=func, scale=scale, **kw), reads=rd, writes=[out])

    def tt(eng, out, in0, in1, op):
        return P.op(eng, lambda e: e.tensor_tensor(out=out, in0=in0, in1=in1, op=op), reads=[in0, in1], writes=[out])

    def ts(eng, out, in0, s1, s2, op0, op1=None):
        rd = [in0] + [s for s in (s1, s2) if isinstance(s, bass.AP)]
        if op1 is None:
            return P.op(eng, lambda e: e.tensor_scalar(out=out, in0=in0, scalar1=s1, scalar2=None, op0=op0), reads=rd, writes=[out])
        return P.op(eng, lambda e: e.tensor_scalar(out=out, in0=in0, scalar1=s1, scalar2=s2, op0=op0, op1=op1), reads=rd, writes=[out])

    def stt(out, in0, scalar, in1, op0, op1):
        rd = [in0, in1] + ([scalar] if isinstance(scalar, bass.AP) else [])
        return P.op("dve", lambda e: e.scalar_tensor_tensor(out=out, in0=in0, scalar=scalar, in1=in1, op0=op0, op1=op1), reads=rd, writes=[out])

    def cp(eng, out, in_):
        if eng == "act":
            return act(out, in_, AF.Copy)
        return P.op(eng, lambda e: e.tensor_copy(out=out, in_=in_), reads=[in_], writes=[out])

    def recip(out, in_):
        return P.op("dve", lambda e: e.reciprocal(out=out, in_=in_), reads=[in_], writes=[out])

    wstate = {"h": 0}
    NH = NS * 2

    def walloc(nel):
        nh = (nel + 2047) // 2048
        if wstate["h"] + nh > NH:
            wstate["h"] = 0
        h = wstate["h"]
        wstate["h"] = (h + nh) % NH
        return h

    def wload(nel, dstb, l, nm, blk):
        h = walloc(nel)
        dst_ap = wbuf[:, h * 2048:h * 2048 + nel]
        dma("sp", dst_ap, dstb[l, :, blk, :], ("w", h), reads=[("wbf", nm, l, blk)], writes=[dst_ap])
        return dst_ap

    conv_chunks = []
    for l in range(L):
        conv_chunks += [(win_d, winb, l, "win", 4, 5), (win_d, winb, l, "win", 0, 2), (win_d, winb, l, "win", 2, 4),
                        (wg_d, wgb, l, "wg", 0, 3), (wg_d, wgb, l, "wg", 3, 6), (wbr_d, wbrb, l, "wbr", 0, 6),
                        (wo_d, wob, l, "wo", 0, 2),
                        (wup_d, wupb, l, "wup", 0, 6), (wup_d, wupb, l, "wup", 6, 12), (wup_d, wupb, l, "wup", 12, 18),
                        (wup_d, wupb, l, "wup", 18, 22), (wdn_d, wdnb, l, "wdn", 0, 4), (wdn_d, wdnb, l, "wdn", 4, 8)]
    cstate = {"n": 0}

    def pump(n):
        for _ in range(n):
            i = cstate["n"]
            if i >= len(conv_chunks):
                return
            src, dst, l, nm, k0, k1 = conv_chunks[i]
            rd = [("cvdone", i - 2)] if i >= 2 else []
            dma("pool", dst[l, :, k0:k1, :], src[l, :, k0:k1, :], ("cv", i), reads=rd,
                writes=[("wbf", nm, l, k) for k in range(k0, k1)] + [("cvdone", i)])
            cstate["n"] += 1

    def ensure(nm, l):
        last = max(i for i, c in enumerate(conv_chunks) if c[3] == nm and c[2] == l)
        while cstate["n"] <= last:
            pump(1)

    CPL = 13

    dma("sp", pcol[:], pcol_d[:, :], "c_pcol", writes=[pcol[:]])
    dma("sp", ident[:], ident_d[:, :], "c_ident", writes=[ident[:]])
    dma("pool", onesm[:], onesm_d[:, :], "c_ones", writes=[onesm[:]])
    dma("pool", blkb[:], blk_d[:, :], "c_blk", writes=[blkb[:]])
    dma("pool", rotb[:], rotm_d[:, :], "c_rot", writes=[rotb[:]])
    ckpt(0)
    ensure("win", 0)
    ckpt(1)
    dma("sp", relrep_s[0:33, :, :], relrep_d[:, :, :], "c_rel", writes=[relrep_s[0:33, :, :]])
    dma("sp", ohm_s[0:33, :, :], ohm_d[:, :, :], "c_ohm", writes=[ohm_s[0:33, :, :]])
    for c in range(3):
        for h in range(8):
            b = bank("L8")
            mm(ps[:, b, 0:256], relrep_s[0:33, h, :], ohm_s[0:33, c, :], True, True)
            cp("dve" if h % 2 == 0 else "act", trep[:, h, :], ps[:, b, 0:256])
        dma("sp", tabx[c, :, :, :], trep[:, :, :], "tabx", reads=[trep[:, :, :]], writes=[("tabx", c)])
        for g in range(2):
            src = bass.AP(tabx, c * 128 * 2048 + g * 4 * 256 + 127, [[2047, 128], [256, 4], [1, 128]])
            dst = biasT[:, c * 2 + g, :].rearrange("p (a b) -> p a b", b=128)
            dma("sp", dst, src, "biasT", reads=[("tabx", c)], writes=[biasT[:, c * 2 + g, :]])
    cp("dve", identb[:], ident[:])
    for i in range(6):
        cp("act", biasHL[:, i, :], biasT[:, i, :])
        cp("dve", bh32[:, 0, :], biasHL[:, i, :])
        tt("dve", bh32[:, 1, :], biasT[:, i, :], bh32[:, 0, :], ALU.subtract)
        cp("act", biasHL[:, 6 + i, :], bh32[:, 1, :])
    for vv in (Vb, Vc):
        for g in range(2):
            a = vv[:, :, g * 128 + 64:g * 128 + 128]
            P.op("pool", lambda e, a=a: e.memset(a, 1.0), writes=[a])
    ckpt(2)

    def rmsnorm(ncols, gname, out3, out_f32=False):
        xT = xTv()
        b = bank("L8")
        for k in range(8):
            s = sq[:, k % 2, 0:ncols]
            xk = xT[:, k, 0:ncols]
            if k % 2 == 0:
                tt("pool", s, xk, xk, ALU.mult)
            else:
                act(s, xk, AF.Square)
            mm(ps[:, b, 0:ncols], onesm[:], s, k == 0, k == 7)
        act(rstd[:, 0:ncols], ps[:, b, 0:ncols], AF.Ln, bias=pc("eps"))
        act(rstd[:, 0:ncols], rstd[:, 0:ncols], AF.Exp, scale=-0.5)
        for k in range(8):
            stt(out3[:, k, 0:ncols], xT[:, k, 0:ncols], pc(gname, k), rstd[:, 0:ncols], ALU.mult, ALU.mult)

    sq8 = av(34 * K, 8 * K, BF16, 512)
    tmpS4 = av(16 * K, 8 * K, F32, 512)

    sq8b = av(26 * K, 8 * K, BF16, 512)

    def norm_sq(p, ncols, sqb=None):
        sqb = sq8 if sqb is None else sqb
        xT = xTv(p)
        for k in range(8):
            s_ = sqb[:, k, 0:ncols]
            xk = xT[:, k, 0:ncols]
            if k % 2 == 0:
                tt("pool", s_, xk, xk, ALU.mult)
            else:
                act(s_, xk, AF.Square)

    def norm_fin(p, ncols, gname, out3, sqb=None):
        sqb = sq8 if sqb is None else sqb
        xT = xTv(p)
        b = bank("L8")
        for k in range(8):
            mm(ps[:, b, 0:ncols], onesm[:], sqb[:, k, 0:ncols], k == 0, k == 7)
        act(rstd[:, 0:ncols], ps[:, b, 0:ncols], AF.Ln, bias=pc("eps"))
        act(rstd[:, 0:ncols], rstd[:, 0:ncols], AF.Exp, scale=-0.5)
        for k in range(8):
            stt(out3[:, k, 0:ncols], xT[:, k, 0:ncols], pc(gname, k), rstd[:, 0:ncols], ALU.mult, ALU.mult)

    def qknorm_rope(src, gcol, out, fill=None):
        def f(n):
            if fill is not None:
                for _ in range(n):
                    fill()
        cp("dve", rt[:, 0, :], src)
        act(sqc, rt[:, 0, :], AF.Square)
        f(1)
        b = bank("L8")
        mm(ps[:, b, :], blkb[:], sqc, True, True)
        act(rt[:, 1, :], ps[:, b, :], AF.Ln, bias=pc("eps"))
        act(rt[:, 1, :], rt[:, 1, :], AF.Exp, scale=-0.5)
        stt(knb, rt[:, 0, :], gcol, rt[:, 1, :], ALU.mult, ALU.mult)
        stt(rt[:, 2, :], rt[:, 0, :], gcol, rt[:, 1, :], ALU.mult, ALU.mult)
        f(1)
        b2 = bank("L8")
        mm(ps[:, b2, :], rotb[:], knb, True, True)
        tt("pool", rt[:, 3, :], rt[:, 2, :], ropeC[:], ALU.mult)
        tt("dve", rt[:, 0, :], ps[:, b2, :], ropeS[:], ALU.mult)
        tt("pool", out, rt[:, 3, :], rt[:, 0, :], ALU.add)

    def load_rope(t):
        dma("sp", ropeC[:], rope_d[0, :, t * T:(t + 1) * T], "ropeC", writes=[ropeC[:]])
        dma("sp", ropeS[:], rope_d[1, :, t * T:(t + 1) * T], "ropeS", writes=[ropeS[:]])

    def load_xT(src, name, lo, ncols, p=None):
        p = cur["p"] if p is None else p
        xT = xTv(p)
        dma("sp", xT[:, :, 0:ncols], src[:, :, lo:lo + ncols], ("xT", p), reads=xs_keys(name, lo, lo + ncols), writes=[xT[:, :, 0:ncols]])

    deferred = []

    def flush():
        while deferred:
            deferred.pop(0)()

    def xs_keys(name, lo, hi):
        return [(name, i) for i in range(lo // 2, (hi - 1) // 2 + 1)]

    for l in range(L):
        dma("sp", lnG[:], lngb_d[l, 0, :, :], "lnG", writes=[lnG[:]])
        dma("sp", lnB[:], lngb_d[l, 1, :, :], "lnB", writes=[lnB[:]])
        dma("sp", bS[:], bS_d[l, :, :, :], "bS", writes=[bS[:]])
        dma("pool", wsT[:], wsT_d[l, :, :, :], "wsT", writes=[wsT[:]])
        dma("sp", sinke[:], sinkb_d[l, :, :, :], "sinke", writes=[sinke[:]])
        act(sinke[:], sinke[:], AF.Exp)
        ckpt(20)

        ensure("win", l)
        wkv = wload(4096, winb, l, "win", 4).rearrange("p (a b) -> p a b", b=512)
        ckpt(21)
        if l > 0:
            load_xT(xsA, "xsA", 0, T)
        for t in range(NTILE):
            c0 = t * T
            xT = xTv()
            if l == 0:
                pump(int(os.environ.get("KPUMP", (CPL + NTILE - 1) // NTILE)))
            ckpt(30)
            if l == 0:
                for bq in range(4):
                    xi = xin[:, bq, :]
                    dma("sp", xi, x_d[c0 + bq * 128:c0 + (bq + 1) * 128, :], ("xin", bq), writes=[xi])
                    for hf in range(2):
                        b = bank("L8")
                        for kk in range(4):
                            k = hf * 4 + kk
                            tr(ps[:, b, kk * 128:(kk + 1) * 128], xi[:, k * 128:(k + 1) * 128])
                        cp("dve" if hf == 0 else "act", xT[:, hf * 4:hf * 4 + 4, bq * 128:(bq + 1) * 128],
                           ps[:, b, :].rearrange("p (a b) -> p a b", b=128))
                dma("sp", xsA[:, :, c0:c0 + T], xT[:, :, :], ("xTst", cur["p"]), reads=[xT[:, :, :]], writes=xs_keys("xsA", c0, c0 + T))
            ckpt(31)
            load_rope(t)
            rmsnorm(T, f"n1g{l}", xn)
            if l > 0 and t + 1 < NTILE:
                load_xT(xsA, "xsA", c0 + T, T, 1 - cur["p"])
            ckpt(32)
            fa = []

            def unit_kb(c0=c0):
                b = bank("L8")
                for k in range(8):
                    mm(ps[:, b, :], wkv[:, k, 0:128], xn[:, k, :], k == 0, k == 7)
                cp("act", KbT[:, c0:c0 + T], ps[:, b, :])

            def unit_v(bq, t=t):
                nb = t * 4 + bq
                b = bank("L8")
                for k in range(8):
                    mm(ps[:, b, 0:256], xn[:, k, bq * 128:(bq + 1) * 128], wkv[:, k, 256:512], k == 0, k == 7)
                cp("dve", Vb[:, nb, :].rearrange("p (g c) -> p g c", c=128)[:, :, 0:64],
                   ps[:, b, 0:128].rearrange("p (g c) -> p g c", c=64))
                cp("dve", Vc[:, nb, :].rearrange("p (g c) -> p g c", c=128)[:, :, 0:64],
                   ps[:, b, 128:256].rearrange("p (g c) -> p g c", c=64))

            fa.append(unit_kb)
            for bq in range(4):
                fa.append(lambda bq=bq: unit_v(bq))

            def fillA():
                if fa:
                    fa.pop(0)()

            b = bank("L8")
            for k in range(8):
                mm(ps[:, b, :], wkv[:, k, 128:256], xn[:, k, :], k == 0, k == 7)
            qknorm_rope(ps[:, b, :], pc(f"kg{l}"), KcT[:, c0:c0 + T], fillA)
            while fa:
                fillA()
            cur["p"] ^= 1

        ckpt(4)
        ensure("wo", l)
        load_xT(xsA, "xsA", 0, T)
        for t in range(NTILE):
            c0 = t * T
            xT = xTv()
            load_rope(t)
            if t == 0:
                rmsnorm(T, f"n1g{l}", xn)
            pre_b = (lambda c0=c0, p=1 - cur["p"]: load_xT(xsA, "xsA", c0 + T, T, p)) if t + 1 < NTILE else None
            wcache = {}
            wcache[0] = wload(4096, winb, l, "win", 0).rearrange("p (a b) -> p a b", b=512)
            wq = wload(4096, winb, l, "win", 2).rearrange("p (a b) -> p a b", b=512)

            def wget(blk):
                if blk not in wcache:
                    wcache[blk] = wload(4096, winb, l, "win", blk).rearrange("p (a b) -> p a b", b=512)
                return wcache[blk]

            def unit_u(j):
                w = wget(0)
                b = bank("L8")
                for k in range(8):
                    mm(ps[:, b, :], w[:, k, j * 128:(j + 1) * 128], xn[:, k, :], k == 0, k == 7)
                act(uT[:, j, :], ps[:, b, :], AF.Gelu_apprx_tanh)

            vslots = [vg[:, 0, :], vg[:, 1, :], tmpS[:, 0, :], tmpS[:, 1, :]]

            def unit_v(bq):
                w = wget(3)
                b = bank("L8")
                for k in range(8):
                    mm(ps[:, b, :], xn[:, k, bq * 128:(bq + 1) * 128], w[:, k, :], k == 0, k == 7)
                act(vslots[bq], ps[:, b, :], AF.Gelu_apprx_tanh)

            def ln_all():
                for bq in range(4):
                    v_ = vslots[bq]
                    so = bq * 8
                    P.op("dve", lambda e, v_=v_, so=so: e.bn_stats(out=stat[:, so:so + 6], in_=v_), reads=[v_], writes=[stat[:, so:so + 6]])
                    P.op("dve", lambda e, so=so: e.bn_aggr(out=stat[:, so + 6:so + 8], in_=stat[:, so:so + 6]),
                         reads=[stat[:, so:so + 6]], writes=[stat[:, so + 6:so + 8]])
                for bq in range(4):
                    so = bq * 8
                    act(stat[:, so + 7:so + 8], stat[:, so + 7:so + 8], AF.Sqrt, bias=pc("eps"))
                for bq in range(4):
                    so = bq * 8
                    recip(stat[:, so + 7:so + 8], stat[:, so + 7:so + 8])
                for bq in range(4):
                    v_ = vslots[bq]
                    so = bq * 8
                    ts("dve", v_, v_, stat[:, so + 6:so + 7], stat[:, so + 7:so + 8], ALU.subtract, ALU.mult)
                    tt("pool", v_, v_, lnG[:], ALU.mult)
                    tt("pool", vln[:, bq, :], v_, lnB[:], ALU.add)

            def unit_qb(j):
                w = wget(1)
                b = bank("L8")
                for k in range(8):
                    mm(ps[:, b, :], w[:, k, j * 128:(j + 1) * 128], xn[:, k, :], k == 0, k == 7)
                act(qbT[:, j, :], ps[:, b, :], AF.Identity, scale=0.125)

            for j in range(4):
                unit_u(j)
            for j in range(4):
                unit_v(j)
            ln_all()
            fb = []
            for j in range(4):
                fb.append(None)
                fb.append(lambda j=j: unit_qb(j))

            def fillB():
                if fb:
                    u_ = fb.pop(0)
                    if u_ is not None:
                        u_()

            for j in range(4):
                b = bank("L8")
                for k in range(8):
                    mm(ps[:, b, :], wq[:, k, j * 128:(j + 1) * 128], xn[:, k, :], k == 0, k == 7)
                qknorm_rope(ps[:, b, :], pc(f"qg{l}"), qcT[:, j, :], fillB)
            while fb:
                fillB()
            flush()
            if pre_b is not None:
                pre_b()
            if l + 1 < L:
                pump((CPL + NTILE - 1) // NTILE)
            ckpt(44)
            for j in range(4):
                b = bank("L8")
                for bq in range(4):
                    for hh in range(2):
                        g = 2 * j + hh
                        mm(ps[hh * 64:(hh + 1) * 64, b, bq * 128:(bq + 1) * 128], vln[:, bq, g * 64:(g + 1) * 64], wsT[:, g, :], True, True)
                tm = tmpS[:, j % 2, :]
                tt("dve", tm.rearrange("p (a b) -> p a b", b=128), ps[:, b, :].rearrange("p (a b) -> p a b", b=128),
                   bS[:, j:j + 1, :].to_broadcast([128, 4, 128]), ALU.add)
                tt("pool", oT[:, j, :], tm, uT[:, j, :], ALU.mult)
            ckpt(45)
            for qi in range(4):
                n = t * 4 + qi
                qs = slice(qi * 128, (qi + 1) * 128)
                chunks = [c for c in (0, 1, 2) if 0 <= n + c - 1 < NB]
                bO = [4, 5] if qi % 2 == 0 else [6, 7]
                nch = len(chunks)

                def SB(ci):
                    kb = n + chunks[ci] - 1
                    c = chunks[ci]
                    for g in range(2):
                        pr = slice(g * 64, (g + 1) * 64)
                        bk = ps[:, (ci % 2) * 2 + g, :]
                        mm(bk, KbT[pr, kb * 128:(kb + 1) * 128], qbT[pr, :, qs], True, False)
                        mm(bk, identb[:], biasHL[:, c * 2 + g, :], False, False)
                        mm(bk, identb[:], biasHL[:, 6 + c * 2 + g, :], False, True)

                def DB(ci):
                    pass

                def EB(ci):
                    c = chunks[ci]
                    edge = (n == HB - 1 and c == 2) or (n == HB and c == 0)
                    s2 = (ci % 2) * 2
                    act(PT[:, s2:s2 + 2, :].rearrange("p a b -> p (a b)"), ps[:, s2:s2 + 2, :].rearrange("p a b -> p (a b)"),
                        AF.Exp, bias=pc("cmask") if edge else pc("zero"))

                def PVB(ci):
                    kb = n + chunks[ci] - 1
                    for g in range(2):
                        mm(ps[:, bO[g], :], Vb[:, kb, g * 128:(g + 1) * 128], PT[:, (ci % 2) * 2 + g, :], ci == 0, ci == nch - 1)

                SB(0)
                if nch > 1:
                    SB(1)
                DB(0)
                for ci in range(nch):
                    EB(ci)
                    PVB(ci)
                    if ci + 2 < nch:
                        SB(ci + 2)
                    if ci + 1 < nch:
                        DB(ci + 1)
                for g in range(2):
                    r_ = rc[64:128, g, :]
                    tt("dve", r_, ps[64:128, bO[g], :], sinke[64:128, g, :], ALU.add)
                    act(r_, r_, AF.Ln)
                    act(r_, r_, AF.Exp, scale=-1.0)
                    tt("dve", oT[g * 64:(g + 1) * 64, 4:8, qs], ps[0:64, bO[g], :].rearrange("p (a b) -> p a b", b=128),
                       r_.rearrange("p (a b) -> p a b", b=128), ALU.mult)
            ckpt(46)
            for qi in range(4):
                n = t * 4 + qi
                qs = slice(qi * 128, (qi + 1) * 128)
                bO = [4, 5] if qi % 2 == 0 else [6, 7]

                def SC(kb):
                    for g in range(2):
                        pr = slice(g * 64, (g + 1) * 64)
                        mm(ps[:, (kb % 2) * 2 + g, :], KcT[pr, kb * 128:(kb + 1) * 128], qcT[pr, :, qs], True, True)

                def EC(kb):
                    cross = (n < HB) != (kb < HB)
                    s2 = (kb % 2) * 2
                    s3 = (kb % 3) * 2
                    act(PT[:, s3:s3 + 2, :].rearrange("p a b -> p (a b)"), ps[:, s2:s2 + 2, :].rearrange("p a b -> p (a b)"),
                        AF.Exp, bias=pc("cmask") if cross else pc("zero"), scale=0.125)

                def PVC(kb):
                    for g in range(2):
                        mm(ps[:, bO[g], :], Vc[:, kb, g * 128:(g + 1) * 128], PT[:, (kb % 3) * 2 + g, :], kb == 0, kb == NB - 1)

                SC(0)
                SC(1)
                for kb in range(NB):
                    EC(kb)
                    if kb + 2 < NB:
                        SC(kb + 2)
                    PVC(kb)
                for g in range(2):
                    r_ = rc[64:128, g, :]
                    recip(r_, ps[64:128, bO[g], :])
                    tt("dve", oT[g * 64:(g + 1) * 64, 8:12, qs], ps[0:64, bO[g], :].rearrange("p (a b) -> p a b", b=128),
                       r_.rearrange("p (a b) -> p a b", b=128), ALU.mult)
            if pre_b is not None:
                norm_sq(1 - cur["p"], T, sq8b)
            ckpt(47)
            for kq in range(2):
                for nbr in range(3):
                    wg_ = wload(4096, wgb, l, "wg", nbr * 2 + kq).rearrange("p (a b) -> p a b", b=512)
                    wb_ = wload(2048, wbrb, l, "wbr", nbr * 2 + kq).rearrange("p (a b) -> p a b", b=512)
                    for kk in range(4):
                        k = kq * 4 + kk
                        cs = slice(kk * 128, (kk + 1) * 128)
                        bY = bank("L8")
                        for j in range(4):
                            mm(ps[:, bY, :], wb_[:, j, cs], oT[:, nbr * 4 + j, :], j == 0, j == 3)
                        bG = bank("L8")
                        for i in range(8):
                            mm(ps[:, bG, :], wg_[:, i, cs], xn[:, i, :], i == 0, i == 7)
                        g_ = gt[:, kk % 2, :]
                        act(g_, ps[:, bG, :], AF.Sigmoid, bias=pc(f"bg{l}", nbr * 8 + k))
                        if nbr == 0:
                            tt("dve", acc[:, kk, :], g_, ps[:, bY, :], ALU.mult)
                        elif nbr == 1:
                            tt("dve", g_, g_, ps[:, bY, :], ALU.mult)
                            tt("pool", acc[:, kk, :], acc[:, kk, :], g_, ALU.add)
                        else:
                            tt("dve", g_, g_, ps[:, bY, :], ALU.mult)
                            tt("pool", mg[:, k, :], acc[:, kk, :], g_, ALU.add)
            if pre_b is not None:
                norm_fin(1 - cur["p"], T, f"n1g{l}", xn, sq8b)
            ckpt(48)
            for kq in range(2):
                w = wload(4096, wob, l, "wo", kq).rearrange("p (a b) -> p a b", b=512)
                for kk in range(4):
                    k = kq * 4 + kk
                    b = bank("L8")
                    for i in range(8):
                        mm(ps[:, b, :], w[:, i, kk * 128:(kk + 1) * 128], mg[:, i, :], i == 0, i == 7)
                    tt("dve", xT[:, k, :], ps[:, b, :], xT[:, k, :], ALU.add)
            deferred.append(lambda c0=c0, xT=xT, p=cur["p"]: dma("sp", xsB[:, :, c0:c0 + T], xT[:, :, :], ("xTst", p), reads=[xT[:, :, :]],
                                                                 writes=xs_keys("xsB", c0, c0 + T)))
            cur["p"] ^= 1
        flush()

        ckpt(5)
        ensure("wdn", l)
        tiles = []
        for hbase in (0, H):
            a = hbase
            while a < hbase + H:
                bnd = min(a + FT, hbase + H)
                tiles.append((a, bnd))
                a = bnd
        def cgeom(a, bnd):
            lo = a - 1 if a > 0 else a
            hi = bnd + 1 if bnd < NT else bnd
            return lo, hi

        lo_, hi_ = cgeom(*tiles[0])
        load_xT(xsB, "xsB", lo_, hi_ - lo_)
        for ti, (a, bnd) in enumerate(tiles):
            lo, hi = cgeom(a, bnd)
            ncols = hi - lo
            oi = a - lo
            wo_ = bnd - a
            x0 = 0 if a > 0 else 1
            y0 = 0 if bnd < NT else 1
            xT = xTv()
            if ti == 0:
                rmsnorm(ncols, f"n2g{l}", xn)
            pre_c = None
            ncn = 0
            if ti + 1 < len(tiles):
                lo_, hi_ = cgeom(*tiles[ti + 1])
                ncn = hi_ - lo_
                pre_c = lambda lo_=lo_, hi_=hi_, p=1 - cur["p"]: load_xT(xsB, "xsB", lo_, hi_ - lo_, p)
            for c in range(NFC):
                wu = wload(2048, wupb, l, "wup", c).rearrange("p (a b) -> p a b", b=256)
                if c == 2:
                    flush()
                if c == 8 and pre_c is not None:
                    pre_c()
                if c == 14 and pre_c is not None:
                    norm_sq(1 - cur["p"], ncn)
                hb_ = []
                for gv in range(2):
                    b = bank("L8")
                    for i in range(8):
                        mm(ps[:, b, 0:ncols], wu[:, i, gv * 128:(gv + 1) * 128], xn[:, i, 0:ncols], i == 0, i == 7)
                    if a == H and a > 0:
                        ts("dve", ps[:, b, 0:1], ps[:, b, 0:1], pc("m01"), None, ALU.mult)
                    if bnd == H:
                        ts("dve", ps[:, b, ncols - 1:ncols], ps[:, b, ncols - 1:ncols], pc("m01"), None, ALU.mult)
                    ch = gv * NFC + c
                    h_ = hc[:, (c % 2) * 2 + gv, :]
                    act(h_[:, 0:wo_], ps[:, b, oi:oi + wo_], AF.Identity, bias=pc(f"cb{l}", ch), scale=pc(f"cw1{l}", ch))
                    if wo_ - x0 > 0:
                        stt(h_[:, x0:wo_], ps[:, b, oi + x0 - 1:oi + wo_ - 1], pc(f"cw0{l}", ch), h_[:, x0:wo_], ALU.mult, ALU.add)
                    if wo_ - y0 > 0:
                        stt(h_[:, 0:wo_ - y0], ps[:, b, oi + 1:oi + 1 + wo_ - y0], pc(f"cw2{l}", ch), h_[:, 0:wo_ - y0], ALU.mult, ALU.add)
                    hb_.append(h_)
                s_ = sg[:, c % 2, :]
                act(s_[:, 0:wo_], hb_[0][:, 0:wo_], AF.Silu)
                tt("pool", actb[:, c, 0:wo_], s_[:, 0:wo_], hb_[1][:, 0:wo_], ALU.mult)
            if pre_c is not None:
                norm_fin(1 - cur["p"], ncn, f"n2g{l}", xn)
            for k in range(8):
                wd = wload(NFC * 128, wdnb, l, "wdn", k).rearrange("p (a b) -> p a b", b=128)
                b = bank("L8")
                for c in range(NFC):
                    mm(ps[:, b, 0:wo_], wd[:, c, :], actb[:, c, 0:wo_], c == 0, c == NFC - 1)
                tt("dve", xT[:, k, oi:oi + wo_], ps[:, b, 0:wo_], xT[:, k, oi:oi + wo_], ALU.add)
            deferred.append(lambda a=a, bnd=bnd, xT=xT, oi=oi, wo_=wo_, p=cur["p"]: dma(
                "sp", xsA[:, :, a:bnd], xT[:, :, oi:oi + wo_], ("xTst", p), reads=[xT[:, :, oi:oi + wo_]], writes=xs_keys("xsA", a, bnd)))
            cur["p"] ^= 1
        flush()

    ckpt(6)
    load_xT(xsA, "xsA", 0, T)
    for t in range(NTILE):
        c0 = t * T
        rmsnorm(T, "fing", yT)
        if t + 1 < NTILE:
            load_xT(xsA, "xsA", c0 + T, T, 1 - cur["p"])
        cur["p"] ^= 1
        for bq in range(4):
            yo_ = yo[:, bq % 2, :]
            for hf in range(2):
                b = bank("L8")
                for kk in range(4):
                    k = hf * 4 + kk
                    tr(ps[:, b, kk * 128:(kk + 1) * 128], yT[:, k, bq * 128:(bq + 1) * 128])
                cp("dve" if hf == 0 else "act", yo_[:, hf * 512:(hf + 1) * 512], ps[:, b, :])
            dma("sp", y_d[c0 + bq * 128:c0 + (bq + 1) * 128, :], yo_, ("yst", bq % 2), reads=[yo_], writes=[("y", t, bq)])

    counts = P.finalize()
    CH = P.CH
    sem_stack = ExitStack()
    esem = {}
    for e in P.ENGS:
        n = (counts[e] + CH - 1) // CH
        esem[e] = [sem_stack.enter_context(nc.semaphore(f"s_{e}_{i}")) for i in range(max(n, 1))]
    dsem = {}
    for k in P.dma_cum:
        dsem[k] = sem_stack.enter_context(nc.semaphore(f"d{len(dsem)}"))

    def emit_engine(ename, eng):
        waited_c = {e: 0 for e in P.ENGS}
        waited_d = {}
        for o in P.ops[ename]:
            need = {}
            for d in o.deps:
                if d.count > need.get(d.eng, 0):
                    need[d.eng] = d.count
            for se, cnt in need.items():
                if cnt > waited_c[se]:
                    eng.wait_ge(esem[se][(cnt - 1) // CH], (cnt - 1) % CH + 1)
                    waited_c[se] = cnt
            for sk, val in o.dma_waits.items():
                if val > waited_d.get(sk, 0):
                    eng.wait_ge(dsem[sk], val)
                    waited_d[sk] = val
            if o.fn is None:
                continue
            ins = o.fn(eng)
            if o.is_dma:
                ins.then_inc(dsem[o.semkey], 16)
            elif o.signal:
                ins.then_inc(esem[ename][(o.count - 1) // CH], 1)
        if ename == "sp":
            for sk, val in P.dma_cum.items():
                eng.wait_ge(dsem[sk], val)

    with nc.Block() as block:
        @block.tensor
        def _(e):
            emit_engine("pe", e)

        @block.scalar
        def _(e):
            emit_engine("act", e)

        @block.vector
        def _(e):
            emit_engine("dve", e)

        @block.gpsimd
        def _(e):
            emit_engine("pool", e)

        @block.sync
        def _(e):
            emit_engine("sp", e)

    sem_stack.close()
    es.close()
    return nc


_PROG_CACHE = {}


def run_cores(core_tokens, two_seq_flags, params, NT, L):
    key = (NT, L)
    if key not in _PROG_CACHE:
        _PROG_CACHE[key] = build_program(NT, L)
    nc = _PROG_CACHE[key]
    shared = _shared_inputs(params, L)
    rope2 = _rope_host(NT, True)
    rope1 = _rope_host(NT, False)
    in_maps = []
    for xt, two in zip(core_tokens, two_seq_flags):
        m = dict(shared)
        m["x"] = np.ascontiguousarray(xt, dtype=np.float32)
        m["pcol"] = _pcol_host(params, L, two)
        m["rope"] = rope2 if two else rope1
        in_maps.append(m)
    res = run_bass_kernel_spmd(nc, in_maps, core_ids=list(range(len(in_maps))))
    return [np.asarray(r["y"]) for r in res.results]


def kernel(x_prompt, x_sample, rel_bias, norm1_g, w_in, ln_v_g, ln_v_b, w_spatial, b_spatial, sink,
           q_norm_g, k_norm_g, w_gate, b_gate, w_branch, w_out, norm2_g, w_up, conv_w, conv_b, w_down, final_g):
    params = dict(rel_bias=rel_bias, norm1_g=norm1_g, w_in=w_in, ln_v_g=ln_v_g, ln_v_b=ln_v_b, w_spatial=w_spatial,
                  b_spatial=b_spatial, sink=sink, q_norm_g=q_norm_g, k_norm_g=k_norm_g, w_gate=w_gate, b_gate=b_gate,
                  w_branch=w_branch, w_out=w_out, norm2_g=norm2_g, w_up=w_up, conv_w=conv_w, conv_b=conv_b,
                  w_down=w_down, final_g=final_g)
    params = {k: np.asarray(v, dtype=np.float32) for k, v in params.items()}
    xp = np.asarray(x_prompt, dtype=np.float32)
    xs = np.asarray(x_sample, dtype=np.float32)
    NT = 4096
    toks = [xp[2 * c:2 * c + 2].reshape(NT, D) for c in range(4)] + [xs[c] for c in range(4)]
    flags = [True] * 4 + [False] * 4
    ys = run_cores(toks, flags, params, NT, 2)
    y_prompt = np.stack(ys[0:4]).reshape(8, 2048, D).astype(np.float32)
    y_sample = np.stack(ys[4:8]).reshape(4, 4096, D).astype(np.float32)
    return (y_prompt, y_sample)
```
